# Optimizing a Trainium2 kernel written in Bass

```python
import math
import jax, jax.numpy as jnp
from jax import lax
import numpy as np

D_MODEL = 1024
BATCH = 8
SEQ = 2048
DEPTH = 2
DEC_BATCH = 128
DEC_SEQ = 1
PAST_LEN = 16384
PAGE_SIZE = 128

N_META = 16
D_CONV = D_MODEL
CONV_W = 3
GLA_HEADS = 4
GLA_KDIM = D_MODEL // 2
GLA_VDIM = D_MODEL
GLA_DK = GLA_KDIM // GLA_HEADS
GLA_DV = GLA_VDIM // GLA_HEADS
GATE_RANK = 16
GATE_NORMALIZER = 16.0
CHUNK = 64
D_FF = -(-8 * D_MODEL // (3 * 256)) * 256
EPS = 1e-6
SPLIT_SIZES = (D_CONV, D_CONV, D_CONV, GLA_KDIM, GLA_KDIM, GLA_VDIM, GLA_VDIM, GATE_RANK, D_MODEL, D_MODEL)
D_IN = sum(SPLIT_SIZES)
SPLIT_IDX = tuple(int(s) for s in np.cumsum(SPLIT_SIZES)[:-1])

kernel_name = "hybrid_shortconv_gla_gated_merge_step"


def rmsnorm(x, g):
    xf = x.astype(jnp.float32)
    y = xf * lax.rsqrt(jnp.mean(xf * xf, axis=-1, keepdims=True) + EPS) * g.astype(jnp.float32)
    return y.astype(x.dtype)


def to_heads(a, d):
    b, t, _ = a.shape
    return a.reshape(b, t, -1, d).transpose(0, 2, 1, 3)


def gla_chunk(S, q, k, v, g):
    L = q.shape[2]
    b = jnp.cumsum(g, axis=2)
    causal = jnp.tril(jnp.ones((L, L), dtype=bool))[:, :, None]
    diff = b[:, :, :, None, :] - b[:, :, None, :, :]
    decay = jnp.where(causal, jnp.exp(jnp.where(causal, diff, 0.0)), 0.0)
    A = jnp.einsum('bhtd,bhsd,bhtsd->bhts', q, k, decay)
    o = jnp.einsum('bhts,bhsv->bhtv', A, v) + jnp.einsum('bhtd,bhdv->bhtv', q * jnp.exp(b), S)
    b_last = b[:, :, -1:, :]
    S_new = jnp.exp(b_last[:, :, 0, :])[..., None] * S + jnp.einsum('bhsd,bhsv->bhdv', k * jnp.exp(b_last - b), v)
    return S_new, o


def gla_scan(S0, q, k, v, g, chunk):
    bsz, h, L, _ = q.shape
    c = min(chunk, L)
    n = -(-L // c)
    pad = n * c - L

    def blocks(a):
        a = jnp.pad(a, ((0, 0), (0, 0), (0, pad), (0, 0)))
        return a.reshape(bsz, h, n, c, a.shape[-1]).transpose(2, 0, 1, 3, 4)

    S, o = lax.scan(lambda s, xs: gla_chunk(s, *xs), S0, (blocks(q), blocks(k), blocks(v), blocks(g)))
    o = o.transpose(1, 2, 0, 3, 4).reshape(bsz, h, n * c, -1)[:, :, :L]
    return S, o


def mixer(xn, conv_prev, S0, n_lead, w_in, conv_w, w_gk2, b_gk2, gla_gain, w_a_out, w_b_out, w_o):
    bsz, T, _ = xn.shape
    proj = xn @ w_in
    gB, gC, h, q, k, v, og, z, ga, gb = jnp.split(proj, SPLIT_IDX, axis=-1)
    u = gC * h
    up = jnp.concatenate([conv_prev.astype(u.dtype), u], axis=1)
    cw = conv_w.astype(u.dtype)
    yconv = cw[0] * up[:, :T] + cw[1] * up[:, 1:T + 1] + cw[2] * up[:, 2:T + 2]
    conv_new = up[:, T:]
    ya = (gB * yconv) @ w_a_out
    gk = jax.nn.log_sigmoid((z @ w_gk2 + b_gk2).astype(jnp.float32)) / GATE_NORMALIZER
    qh = to_heads(q, GLA_DK).astype(jnp.float32) * (GLA_DK ** -0.5)
    kh = to_heads(k, GLA_DK).astype(jnp.float32)
    vh = to_heads(v, GLA_DV).astype(jnp.float32)
    gh = to_heads(gk, GLA_DK)
    S0 = S0.astype(jnp.float32)
    if n_lead > 0:
        S, o_lead = gla_scan(S0, qh[:, :, :n_lead], kh[:, :, :n_lead], vh[:, :, :n_lead], gh[:, :, :n_lead], n_lead)
        S, o_rest = gla_scan(S, qh[:, :, n_lead:], kh[:, :, n_lead:], vh[:, :, n_lead:], gh[:, :, n_lead:], CHUNK)
        o = jnp.concatenate([o_lead, o_rest], axis=2)
    else:
        S, o = gla_scan(S0, qh, kh, vh, gh, CHUNK)
    o = rmsnorm(o, gla_gain)
    o = o.transpose(0, 2, 1, 3).reshape(bsz, T, GLA_VDIM).astype(xn.dtype)
    yb = (o * jax.nn.silu(og)) @ w_b_out
    m = jax.nn.sigmoid(ga) * ya + jax.nn.sigmoid(gb) * yb
    return m @ w_o, conv_new, S


def swiglu(xn, w_gu, w_down):
    g, u = jnp.split(xn @ w_gu, 2, axis=-1)
    return (jax.nn.silu(g) * u) @ w_down


def trunk(x, conv_states, gla_states, n_lead, w_in, conv_w, w_gk2, b_gk2, gla_gain,
          w_a_out, w_b_out, w_o, norm_mix, norm_ffn, w_gu, w_down, final_norm):
    conv_out, gla_out = [], []
    for l in range(DEPTH):
        mo, cs, gs = mixer(rmsnorm(x, norm_mix[l]), conv_states[l], gla_states[l], n_lead,
                           w_in[l], conv_w[l], w_gk2[l], b_gk2[l], gla_gain[l],
                           w_a_out[l], w_b_out[l], w_o[l])
        x = x + mo
        x = x + swiglu(rmsnorm(x, norm_ffn[l]), w_gu[l], w_down[l])
        conv_out.append(cs)
        gla_out.append(gs)
    return rmsnorm(x, final_norm), jnp.stack(conv_out), jnp.stack(gla_out)


def setup_inputs(seed: int = 0) -> dict:
    key = jax.random.key(seed)
    ks = jax.random.split(key, 20)
    nrm = lambda k, shape, s: jax.random.normal(k, shape, jnp.float32) * s
    return {
        "x_prompt": nrm(ks[0], (BATCH, SEQ, D_MODEL), 1.0),
        "x_sample": nrm(ks[1], (DEC_BATCH, DEC_SEQ, D_MODEL), 1.0),
        "state_conv": nrm(ks[2], (DEPTH, DEC_BATCH, CONV_W - 1, D_CONV), 1.0),
        "state_gla": nrm(ks[3], (DEPTH, DEC_BATCH, GLA_HEADS, GLA_DK, GLA_DV), 0.1),
        "meta_tokens": nrm(ks[4], (N_META, D_MODEL), 1.0),
        "w_in": nrm(ks[5], (DEPTH, D_MODEL, D_IN), D_MODEL ** -0.5),
        "conv_w": nrm(ks[6], (DEPTH, CONV_W, D_CONV), CONV_W ** -0.5),
        "w_gk2": nrm(ks[7], (DEPTH, GATE_RANK, GLA_KDIM), GATE_RANK ** -0.5),
        "b_gk2": nrm(ks[8], (DEPTH, GLA_KDIM), 0.1),
        "gla_gain": 1.0 + nrm(ks[9], (DEPTH, GLA_DV), 0.01),
        "w_a_out": nrm(ks[10], (DEPTH, D_CONV, D_MODEL), D_CONV ** -0.5),
        "w_b_out": nrm(ks[11], (DEPTH, GLA_VDIM, D_MODEL), GLA_VDIM ** -0.5),
        "w_o": nrm(ks[12], (DEPTH, D_MODEL, D_MODEL), D_MODEL ** -0.5),
        "norm_mix": 1.0 + nrm(ks[13], (DEPTH, D_MODEL), 0.01),
        "norm_ffn": 1.0 + nrm(ks[14], (DEPTH, D_MODEL), 0.01),
        "w_gu": nrm(ks[15], (DEPTH, D_MODEL, 2 * D_FF), D_MODEL ** -0.5),
        "w_down": nrm(ks[16], (DEPTH, D_FF, D_MODEL), D_FF ** -0.5),
        "final_norm": 1.0 + nrm(ks[17], (D_MODEL,), 0.01),
    }


def reference(x_prompt, x_sample, state_conv, state_gla, meta_tokens, w_in, conv_w, w_gk2, b_gk2,
              gla_gain, w_a_out, w_b_out, w_o, norm_mix, norm_ffn, w_gu, w_down, final_norm):
    weights = (w_in, conv_w, w_gk2, b_gk2, gla_gain, w_a_out, w_b_out, w_o,
               norm_mix, norm_ffn, w_gu, w_down, final_norm)
    bsz = x_prompt.shape[0]
    meta = jnp.broadcast_to(meta_tokens.astype(x_prompt.dtype)[None], (bsz, N_META, D_MODEL))
    xp = jnp.concatenate([meta, x_prompt], axis=1)
    conv0 = jnp.zeros((DEPTH, bsz, CONV_W - 1, D_CONV), x_prompt.dtype)
    gla0 = jnp.zeros((DEPTH, bsz, GLA_HEADS, GLA_DK, GLA_DV), jnp.float32)
    yp, p_conv, p_gla = trunk(xp, conv0, gla0, N_META, *weights)
    y_prompt = yp[:, N_META:]
    y_sample, s_conv, s_gla = trunk(x_sample, state_conv, state_gla, 0, *weights)
    return (y_prompt, y_sample, p_conv, p_gla.astype(x_prompt.dtype),
            s_conv.astype(state_conv.dtype), s_gla.astype(state_gla.dtype))
```

```python
import numpy as np
import concourse.bass as bass
import concourse.mybir as mybir
from concourse.bass_utils import run_bass_kernel_spmd
from contextlib import ExitStack

F32 = mybir.dt.float32
BF16 = mybir.dt.bfloat16
AF = mybir.ActivationFunctionType
ALU = mybir.AluOpType

D = 1024
KC = 8
DFF = 2816
NJ = 22
EPS = 1e-6
NCA = 1072
NCORE = 8
C_GB, C_GC, C_H, C_Q, C_K, C_V, C_OG, C_Z, C_GA, C_GBB = 0, 1024, 2048, 3072, 3584, 4096, 5120, 6144, 6160, 7184
WSLOT = 3072

O_ID, O_TRIR, O_TUR, O_TRIE, O_TUE, O_MCR, O_MCE, O_DELTA = 0, 128, 256, 384, 512, 640, 768, 896
O_EPS = O_DELTA + 16 * 48
O_NH = O_EPS + 1
O_ONE = O_NH + 1
NCF = O_ONE + 1


def make_consts():
    c = np.zeros((128, NCF), np.float32)
    idx = np.arange(128)
    c[:, O_ID:O_ID + 128] = np.eye(128, dtype=np.float32)
    s = idx[:, None]
    t = idx[None, :]
    g = -1.0 / 16.0
    c[:, O_TRIR:O_TRIR + 128] = np.where(s <= t, g, 0.0)
    c[:, O_TUR:O_TUR + 128] = np.where(s > t, g, 0.0)
    trie = np.zeros((128, 128), np.float32)
    tue = np.zeros((128, 128), np.float32)
    mce = np.zeros((128, 128), np.float32)
    for i in range(16):
        trie[i, i] = g
    for a in range(32, 48):
        for b in range(32, 48):
            if a <= b:
                trie[a, b] = g
                mce[a, b] = 1.0
            if a > b:
                tue[a, b] = g
    c[:, O_TRIE:O_TRIE + 128] = trie
    c[:, O_TUE:O_TUE + 128] = tue
    c[:, O_MCR:O_MCR + 128] = np.where(s <= t, 1.0, 0.0)
    c[:, O_MCE:O_MCE + 128] = mce
    dl = np.zeros((16, 48), np.float32)
    for i in range(16):
        dl[i, i] = 1.0
    c[:, O_DELTA:O_DELTA + 768] = dl.reshape(1, 768)
    c[:, O_EPS] = EPS
    c[:, O_NH] = -0.5
    c[:, O_ONE] = 1.0
    return c


class Res:
    __slots__ = ("name", "w", "r", "dsem", "dcnt", "dkey", "strict")

    def __init__(self, name):
        self.name = name
        self.w = None
        self.r = {}
        self.dsem = None
        self.dcnt = 0
        self.dkey = None
        self.strict = False


class Sched:
    def __init__(self, nc, es):
        self.nc = nc
        self.es = es
        self.eng = {"pe": nc.tensor, "act": nc.scalar, "dve": nc.vector, "pool": nc.gpsimd, "sp": nc.sync}
        self.sem = {k: es.enter_context(nc.semaphore("s_" + k)) for k in self.eng}
        self.cnt = {k: 0 for k in self.eng}
        self.seen = {k: {} for k in self.eng}
        self.semobj = dict(self.sem)
        self.out_events = []
        self.nwait = 0
        self.pe_time = 0.0
        self.strict_all = True

    def _deps(self, e, reads, writes):
        deps = {}

        def add(ev, own_ok):
            k, v = ev
            if k == e and not own_ok and not self.strict_all:
                return
            if k == e and e in ("pe", "sp"):
                return
            if deps.get(k, 0) < v:
                deps[k] = v

        for r in reads:
            if r.w is not None:
                add(r.w, True)
        for w in writes:
            if w.w is not None:
                add(w.w, w.strict)
            for k, v in w.r.items():
                add((k, v), False)
        return deps

    def _wait(self, e, deps):
        for k, v in deps.items():
            if self.seen[e].get(k, 0) >= v:
                continue
            self.eng[e].wait_ge(self.semobj[k], v)
            self.seen[e][k] = v
            self.nwait += 1

    def _mark(self, ev, reads, writes):
        k, v = ev
        for r in reads:
            if r.r.get(k, 0) < v:
                r.r[k] = v
        for w in writes:
            w.w = ev
            w.r = {}

    def op(self, e, fn, reads=(), writes=(), cost=None):
        self._wait(e, self._deps(e, reads, writes))
        if e == "pe":
            self.pe_time += 0.06 if cost is None else cost
        ins = fn()
        self.cnt[e] += 1
        ins.then_inc(self.sem[e], 1)
        self._mark((e, self.cnt[e]), reads, writes)

    def dma(self, q, res, pairs, reads=(), writes=(), is_output=False, **kw):
        self._wait(q, self._deps(q, reads, writes))
        key = res.dkey if res.dkey is not None else ("d", res.name)
        if res.dsem is None:
            res.dsem = self.es.enter_context(self.nc.semaphore("d_" + res.name))
        self.semobj[key] = res.dsem
        for (o, i) in pairs:
            self.eng[q].dma_start(out=o, in_=i, **kw).then_inc(res.dsem, 16)
            res.dcnt += 16
        ev = (key, res.dcnt)
        self._mark(ev, reads, writes)
        if is_output:
            self.out_events.append(ev)

    def finish(self):
        deps = {}
        for k, v in self.out_events:
            if deps.get(k, 0) < v:
                deps[k] = v
        for e in ("pe", "act", "dve", "pool"):
            if self.cnt[e] > 0:
                deps[e] = self.cnt[e]
        self._wait("sp", deps)


class Half:
    def __init__(self, name):
        self.name = name
        if name == "A":
            self.ncols = NCA
            self.blocks = [(0, 0, 48)] + [(r, 48 + 128 * (r - 1), 128) for r in range(1, 9)]
            self.tiles = [(0, 512), (512, 512), (1024, 48)]
            self.tok0 = 0
        else:
            self.ncols = 1024
            self.blocks = [(r, 128 * (r - 1), 128) for r in range(1, 9)]
            self.tiles = [(0, 512), (512, 512)]
            self.tok0 = 1024

    def blk_of(self, c0, n):
        out = []
        for (bi, b0, rows) in self.blocks:
            if b0 < c0 + n and c0 < b0 + rows:
                out.append(bi)
        return out


def build_nc(depth=2, halves=("A", "B"), dbg=None):
    nc = bass.Bass("TRN2", target_bir_lowering=False)

    def din(name, shape):
        return nc.dram_tensor(name, shape, F32, kind="ExternalInput").ap()

    def dout(name, shape):
        return nc.dram_tensor(name, shape, F32, kind="ExternalOutput").ap()

    x_prompt = din("x_prompt", [2048, D])
    x_sample = din("x_sample", [16, D])
    state_conv = din("state_conv", [2, 16, 2, D])
    state_gla = din("state_gla", [2, 16, 4, 128, 256])
    meta = din("meta_tokens", [16, D])
    w_in = din("w_in", [2, D, 8208])
    conv_wt = din("conv_wt", [128, 48])
    wgk_aug = din("wgk_aug", [17, 1024])
    gla_gain = din("gla_gain", [1, 512])
    w_a = din("w_a_out", [2, D, D])
    w_b = din("w_b_out", [2, D, D])
    w_o = din("w_o", [2, D, D])
    norm_mix = din("norm_mix", [2, D])
    norm_ffn = din("norm_ffn", [2, D])
    w_gu = din("w_gu", [2, D, 2 * DFF])
    w_down = din("w_down", [2, DFF, D])
    final_norm = din("final_norm", [1, D])
    cst_d = din("consts", [128, NCF])
    normw_d = din("normw_t", [128, 32])

    y_prompt = dout("y_prompt", [2048, D])
    y_sample = dout("y_sample", [16, D])
    p_conv = dout("p_conv", [2, 2, D])
    p_gla = dout("p_gla", [2, 4, 128, 256])
    s_conv = dout("s_conv", [2, 16, 2, D])
    s_gla = dout("s_gla", [2, 16, 4, 128, 256])
    dbg_out = {}
    if dbg:
        for name, shape in dbg.items():
            dbg_out[name] = dout("dbg_" + name, shape)

    es = ExitStack()
    S = Sched(nc, es)

    off = [16384]

    def alloc(name, shape, dt):
        nbytes = int(np.prod(shape[1:])) * (4 if dt == F32 else 2)
        o = (off[0] + 31) // 32 * 32
        t = nc.alloc_sbuf_tensor_at(name, list(shape), dt, offset=o)
        off[0] = o + nbytes
        return t

    def alloc_at(name, shape, dt, o):
        return nc.alloc_sbuf_tensor_at(name, list(shape), dt, offset=o)

    x_tm = alloc("x_tm", [128, 9, D], F32)
    xnT = alloc("xnT", [128, KC, NCA], BF16)
    R_off = (off[0] + 31) // 32 * 32
    SLAB = NCA * 2
    Rt = alloc("R", [128, NJ, NCA], BF16)
    NCELL, CELL = 30, 512
    wring = alloc("wring", [128, NCELL * CELL], BF16)
    gbc = [alloc("gbc%d" % i, [128, D], F32) for i in range(2)]
    xn_bf = [alloc("xn_bf%d" % i, [128, D], BF16) for i in range(3)]
    sqj = alloc("sqj", [128, D], BF16)
    Sst = alloc("Sst", [128, 2, 4, 256], F32)
    S_bf = [alloc("S_bf%d" % i, [128, 256], BF16) for i in range(2)]
    cst = alloc("cst", [128, NCF], F32)
    ident_bf = alloc("ident_bf", [128, 128], BF16)
    cwT = alloc("cwT", [128, 48], F32)
    wgk = alloc("wgk", [128, 1024], F32)
    gain_bc = alloc("gain_bc", [128, 512], F32)
    stat = alloc("stat", [128, 64], F32)
    normw = alloc("normw", [128, 32], F32)
    carry = alloc("carry", [128, 2, 8, 2], F32)
    zT = alloc("zT", [128, NCA], BF16)
    wgk_bf = alloc("wgk_bf", [128, 1024], BF16)
    G2_off = (off[0] + 31) // 32 * 32
    G2N = 6144
    G2 = alloc("G2", [128, G2N], F32)
    Ssamp = [alloc("Ssamp%d" % i, [128, 2, 256], F32) for i in range(2)]
    Vb = [alloc("Vb%d" % i, [128, 2, 256], BF16) for i in range(2)]
    Qm = alloc("Qm", [128, 16, 48], BF16)
    Ssamp_bf = [alloc("Ssamp_bf%d" % i, [128, 2, 256], BF16) for i in range(2)]
    assert off[0] <= 16384 + 212000, off[0]

    def slab(i):
        return R_off + i * SLAB

    g_off = [slab(8)]
    g_end = slab(22)

    def galloc(name, shape, dt):
        nbytes = int(np.prod(shape[1:])) * (4 if dt == F32 else 2)
        o = (g_off[0] + 31) // 32 * 32
        assert o + nbytes <= g_end, (name, o + nbytes - g_end)
        g_off[0] = o + nbytes
        return alloc_at(name, shape, dt, o)

    l_h = galloc("l_h", [128, 9, 128], F32)
    EbT = galloc("EbT", [128, NCA], F32)
    ENbT = galloc("ENbT", [128, NCA], F32)
    Ec = galloc("Ec", [128, 9, 128], F32)
    tmpG = [alloc_at("tmpG%d" % i, [128, 512], F32, Ec.manual_sbuf_range[0] + i * 2048) for i in range(2)]

    class GSet:
        pass
    gsets = [GSet(), GSet()]
    gsets[0].qtil = galloc("qtil0", [128, NCA], BF16)
    gsets[0].ktil = galloc("ktil0", [128, NCA], BF16)
    gsets[0].khat = galloc("khat0", [128, 9, 128], BF16)
    gsets[0].v = galloc("v0", [128, 9, 256], BF16)
    g2 = [G2_off]
    g2_end = G2_off + G2N * 4

    def g2alloc(name, shape, dt):
        nbytes = int(np.prod(shape[1:])) * (4 if dt == F32 else 2)
        o = (g2[0] + 31) // 32 * 32
        assert o + nbytes <= g2_end, (name, o + nbytes - g2_end)
        g2[0] = o + nbytes
        return alloc_at(name, shape, dt, o)

    gsets[0].gsog = g2alloc("gsog0", [128, 9, 256], BF16)
    gsets[1].qtil = g2alloc("qtil1", [128, NCA], BF16)
    gsets[1].ktil = g2alloc("ktil1", [128, NCA], BF16)
    gsets[1].khat = g2alloc("khat1", [128, 9, 128], BF16)
    gsets[1].v = g2alloc("v1", [128, 9, 256], BF16)
    gsets[1].gsog = g2alloc("gsog1", [128, 9, 256], BF16)
    for i_ in range(2):
        gsets[i_].dec = g2alloc("dec%d" % i_, [128, 32], F32)
        gsets[i_].q32s = g2alloc("q32s%d" % i_, [128, 16], F32)
    ATs = [g2alloc("ATs%d" % i, [128, 128], BF16) for i in range(2)]
    pb_tm = [g2alloc("pb_tm%d" % i, [128, 256], BF16) for i in range(2)]
    tmp_g = [g2alloc("tmp_g%d" % i, [128, 256], F32) for i in range(2)]
    c_off = [slab(16)]

    def calloc(name, shape, dt):
        nbytes = int(np.prod(shape[1:])) * (4 if dt == F32 else 2)
        o = (c_off[0] + 31) // 32 * 32
        assert o + nbytes <= g_end, (name, o + nbytes - g_end)
        c_off[0] = o + nbytes
        return alloc_at(name, shape, dt, o)

    U = calloc("U", [128, 2 + NCA], F32)
    Ybuf = [calloc("Y%d" % i, [128, 512], F32) for i in range(2)]
    gcb = [calloc("gcb%d" % i, [128, 512], F32) for i in range(2)]
    g2c = [G2_off]

    def g2calloc(name, shape, dt):
        nbytes = int(np.prod(shape[1:])) * (4 if dt == F32 else 2)
        o = (g2c[0] + 31) // 32 * 32
        assert o + nbytes <= g2_end, (name, o + nbytes - g2_end)
        g2c[0] = o + nbytes
        return alloc_at(name, shape, dt, o)

    sc_tm = g2calloc("sc_tm", [128, 2, D], F32)
    prevT = g2calloc("prevT", [128, 16, 16], F32)
    Ys = g2calloc("Ys", [128, 16], F32)
    sgb = [alloc_at("sgb%d" % i, [128, 512], F32, slab(16) + i * 2048) for i in range(2)]
    sgf = [alloc_at("sgf%d" % i, [128, 512], F32, G2_off + i * 2048) for i in range(2)]
    us_tm = sc_tm
    pc_tm = alloc_at("pc_tm", [128, D], F32, G2_off + 4096)

    def mk(name, n):
        return [Res("%s%d" % (name, i)) for i in range(n)]

    x_res = mk("x", 9)
    xn_res = mk("xn", 9)
    slab_res = mk("slab", NJ)
    gbc_res = mk("gbc", 2)
    xnbf_res = mk("xnbf", 3)
    sqj_res = Res("sqj")
    sqj_res.strict = True
    S_res = [[Res("S%d_%d" % (l, h)) for h in range(4)] for l in range(2)]
    Sbf_res = mk("Sbf", 2)
    cst_res = Res("cst")
    idbf_res = Res("idbf")
    stat_res = mk("stat", 8)
    carry_res = [[Res("carry%d_%d" % (l, j)) for j in range(8)] for l in range(2)]
    zT_res = Res("zT")
    Ssamp_res = mk("Ssamp", 2)
    Vb_res = mk("Vb", 2)
    Qm_res = Res("Qm")
    Ssbf_res = mk("Ssbf", 2)
    bank_res = mk("bank", 8)
    g2_res = mk("g2r", G2N * 4 // 1024)

    def cells_of(t, lo_elem=None, hi_elem=None):
        lo, hi = t.manual_sbuf_range
        out = []
        if lo >= R_off and lo < R_off + NJ * SLAB:
            for i in range(NJ):
                a, b = slab(i), slab(i + 1)
                if a < hi and lo < b:
                    out.append(slab_res[i])
        elif lo >= G2_off and lo < g2_end:
            for i in range(G2N * 4 // 1024):
                a, b = G2_off + i * 1024, G2_off + (i + 1) * 1024
                if a < hi and lo < b:
                    out.append(g2_res[i])
        else:
            raise ValueError
        return out

    banks = [nc.alloc_psum_tensor("ps%d" % i, [128, 512], F32) for i in range(8)]
    banks_bf = [b.bitcast(BF16) for b in banks]
    pst = {"i": 0}

    def psum():
        i = pst["i"]
        pst["i"] = (i + 1) % 7
        return i

    def pe_warm(n=1):
        for _ in range(n):
            tns.matmul(banks[6][:, 0:128], lhsT=ident_bf[:, :], rhs=ident_bf[:, :], start=True, stop=True)
            S.pe_time += 0.06

    PE, ACT, DVE, POOL, SP = "pe", "act", "dve", "pool", "sp"
    tns, vec, sca, gps = nc.tensor, nc.vector, nc.scalar, nc.gpsimd

    def ccol(o, n=1, rows=128):
        return cst[0:rows, o:o + n]

    S.dma(SP, cst_res, [(cst[:, :], cst_d[:, :]), (cwT[:, :], conv_wt[:, :]), (wgk[0:17, :], wgk_aug[:, :]), (normw[:, :], normw_d[:, :]),
                        (gain_bc[:, :], gla_gain[0:1, :].broadcast_to([128, 512]))],
          writes=[cst_res])
    S.op(DVE, lambda: vec.tensor_copy(out=ident_bf[:, :], in_=cst[:, O_ID:O_ID + 128]), reads=[cst_res], writes=[idbf_res])
    S.op(DVE, lambda: vec.memset(stat[:, :], 1.0), writes=stat_res)
    S.op(DVE, lambda: vec.memset(zT[0:32, :], 1.0), writes=[zT_res])
    wgkbf_res = Res("wgkbf")
    KM_res = Res("KM")
    S.op(DVE, lambda: vec.tensor_copy(out=wgk_bf[0:17, :], in_=wgk[0:17, :]), reads=[cst_res, KM_res], writes=[wgkbf_res])
    KM = nc.alloc_sbuf_tensor_at("KM", [128, 16, 128], BF16, offset=wgk.manual_sbuf_range[0])

    wctr = {"n": 0, "pos": 0, "owner": [None] * NCELL, "sems": [None] * 16}

    class WT:
        pass

    def wload(pieces, kc):
        n = wctr["n"]
        wctr["n"] += 1
        ntot = sum(p.shape[1] for p in pieces)
        nel = kc * ntot
        ncell = (nel + CELL - 1) // CELL
        if wctr["pos"] + ncell > NCELL:
            wctr["pos"] = 0
        pos = wctr["pos"]
        wctr["pos"] = pos + ncell
        res = Res("w%d" % n)
        acc = {}
        for c in range(pos, pos + ncell):
            o = wctr["owner"][c]
            if o is not None:
                evs = dict(o.r)
                if o.w is not None:
                    evs[o.w[0]] = max(evs.get(o.w[0], 0), o.w[1])
                for k_, v_ in evs.items():
                    if acc.get(k_, 0) < v_:
                        acc[k_] = v_
            wctr["owner"][c] = res
        res.r = acc
        si = n % 16
        if wctr["sems"][si] is None:
            wctr["sems"][si] = [es.enter_context(nc.semaphore("d_w%d" % si)), 0]
        res.dsem, res.dcnt, res.dkey = wctr["sems"][si][0], wctr["sems"][si][1], ("d", "wsem%d" % si)
        dv = wring[:, pos * CELL:pos * CELL + nel].rearrange("p (k c) -> p k c", k=kc)
        pairs = []
        c = 0
        for p in pieces:
            w = p.shape[1]
            pairs.append((dv[:, :, c:c + w], p.rearrange("(k p) c -> p k c", p=128)))
            c += w
        S.dma(POOL, res, pairs, writes=[res])
        wctr["sems"][si][1] = res.dcnt
        w = WT()
        w.ap = dv
        w.res = res
        return w

    def mm_group(bank, rows, n, lhs_list, rhs_list, reads, c0=0):
        def fn():
            ins = None
            m = len(lhs_list)
            for i in range(m):
                ins = tns.matmul(banks[bank][0:rows, c0:c0 + n], lhsT=lhs_list[i], rhs=rhs_list[i],
                                 start=(i == 0), stop=(i == m - 1))
            return ins
        S.op(PE, fn, reads=reads, writes=[bank_res[bank]], cost=len(lhs_list) * max(n, 64) / 2400.0)

    def load_x(H):
        b = None
        if H.name == "A":
            S.op(DVE, lambda: vec.memset(x_tm[0:48, 0, :], 0.0), writes=[x_res[0]])
            S.dma(SP, x_res[0], [(x_tm[0:16, 0, :], x_sample[:, :]), (x_tm[32:48, 0, :], meta[:, :])], writes=[x_res[0]])
        for r in range(1, 9):
            t0 = H.tok0 + 128 * (r - 1)
            S.dma(SP, x_res[r], [(x_tm[:, r, :], x_prompt[t0:t0 + 128, :])], writes=[x_res[r]])

    gst = {"i": 0, "xb": 0}

    nst_res = mk("nst", 9)

    def load_g(gvec):
        gi = gst["i"]
        gst["i"] ^= 1
        S.dma(SP, gbc_res[gi], [(gbc[gi][:, :], gvec.broadcast_to([128, D]))], writes=[gbc_res[gi]])
        return gi

    def norm_block_a(H, bi, c0, rows, gi, final):
        ss, var, lnv, rstd = (stat[0:rows, o + bi:o + bi + 1] for o in (0, 9, 18, 27))
        S.op(ACT, lambda: sca.activation(out=sqj[0:rows, :], in_=x_tm[0:rows, bi, :], func=AF.Square, accum_out=ss),
             reads=[x_res[bi]], writes=[sqj_res, nst_res[bi]])
        S.op(ACT, lambda: sca.activation(out=lnv, in_=ss, func=AF.Ln, scale=1.0 / D, bias=ccol(O_EPS, 1, rows)),
             reads=[nst_res[bi], cst_res], writes=[nst_res[bi]])
        S.op(ACT, lambda: sca.activation(out=rstd, in_=lnv, func=AF.Exp, scale=-0.5), reads=[nst_res[bi]], writes=[nst_res[bi]])
        if final:
            S.op(DVE, lambda: vec.scalar_tensor_tensor(out=x_tm[0:rows, bi, :], in0=x_tm[0:rows, bi, :], scalar=rstd, in1=gbc[gi][0:rows, :],
                                                       op0=ALU.mult, op1=ALU.mult),
                 reads=[x_res[bi], nst_res[bi], gbc_res[gi]], writes=[x_res[bi]])
            if bi == 0:
                S.dma(ACT, x_res[0], [(y_sample[:, :], x_tm[0:16, 0, :])], reads=[x_res[0]], is_output=True)
            else:
                t0 = H.tok0 + 128 * (bi - 1)
                S.dma(ACT, x_res[bi], [(y_prompt[t0:t0 + 128, :], x_tm[:, bi, :])], reads=[x_res[bi]], is_output=True)
            return None
        xb = gst["xb"]
        gst["xb"] = (xb + 1) % 3
        S.op(ACT, lambda: sca.activation(out=xn_bf[xb][0:rows, :], in_=x_tm[0:rows, bi, :], func=AF.Copy, scale=rstd),
             reads=[x_res[bi], nst_res[bi]], writes=[xnbf_res[xb]])
        return xb

    def norm_block_b(bi, c0, rows, xb, wj):
        bk = psum()

        def fn():
            ins = None
            for k in range(KC):
                ins = tns.transpose(out=banks_bf[bk][:, k * 128:k * 128 + rows], in_=xn_bf[xb][0:rows, k * 128:(k + 1) * 128],
                                    identity=ident_bf[0:rows, 0:rows])
            return ins
        S.op(PE, fn, reads=[xnbf_res[xb], idbf_res], writes=[bank_res[bk]], cost=0.06 * KC)
        src = banks_bf[bk][:, :].rearrange("p (k c) -> p k c", k=KC)[:, :, 0:rows]
        gT = normw[:, wj * 8:(wj + 1) * 8].unsqueeze(2).broadcast_to([128, KC, rows])
        S.op(DVE, lambda: vec.tensor_tensor(out=xnT[:, :, c0:c0 + rows], in0=src, in1=gT, op=ALU.mult),
             reads=[bank_res[bk], cst_res], writes=[xn_res[bi]])

    class NormPipe:
        def __init__(self, H, gvec, final=False, depth=2, wj=None):
            self.H, self.final, self.depth, self.wj = H, final, depth, wj
            self.gi = load_g(gvec) if final else None
            self.pend = []

        def block(self, bi, c0, rows):
            xb = norm_block_a(self.H, bi, c0, rows, self.gi, self.final)
            if not self.final:
                self.pend.append((bi, c0, rows, xb, self.wj))
            while len(self.pend) > self.depth:
                norm_block_b(*self.pend.pop(0))

        def flush(self):
            while self.pend:
                norm_block_b(*self.pend.pop(0))

    def norm_phase(H, gvec, wj):
        npipe = NormPipe(H, gvec, wj=wj)
        for (bi, c0, rows) in H.blocks:
            npipe.block(bi, c0, rows)
        npipe.flush()

    def xn_reads(H, c0, n):
        return [xn_res[b] for b in H.blk_of(c0, n)]

    def alias_begin(newres, oldres):
        acc = {}
        for o in oldres:
            if o.w is not None:
                k, v = o.w
                if acc.get(k, 0) < v:
                    acc[k] = v
            for k, v in o.r.items():
                if acc.get(k, 0) < v:
                    acc[k] = v
        for n_ in newres:
            n_.w = None
            n_.r = dict(acc)

    def alias_end(newres, oldres):
        acc = {}
        for o in newres:
            if o.w is not None:
                k, v = o.w
                if acc.get(k, 0) < v:
                    acc[k] = v
            for k, v in o.r.items():
                if acc.get(k, 0) < v:
                    acc[k] = v
        for o in oldres:
            for k, v in acc.items():
                if o.r.get(k, 0) < v:
                    o.r[k] = v

    def gla_phase(H, l):
        isA = H.name == "A"
        newres = []

        def mkr(name, n):
            rs = mk(name, n)
            newres.extend(rs)
            return rs
        lh_res, eb_res, ec_res = mkr("lh", 9), mkr("eb", 9), mkr("ec", 9)
        at_res, pb_res, tg_res = mkr("at", 2), mkr("pb", 2), mkr("tg", 2)
        tgG_res = mkr("tgG", 2)
        for i_, st in enumerate(gsets):
            st.q_res, st.kh_res, st.v_res, st.g_res = mkr("q%d_" % i_, 9), mkr("kh%d_" % i_, 9), mkr("v%d_" % i_, 9), mkr("g%d_" % i_, 9)
            st.dec_res = mkr("dec%d_" % i_, 1)[0]
            st.q32_res = mkr("q32%d_" % i_, 1)[0]
        oldres = slab_res[8:22] + g2_res
        alias_begin(newres, oldres)

        wz = wload([w_in[l, :, C_Z:C_Z + 16]], KC)
        for (c0, n) in H.tiles:
            bk = psum()
            mm_group(bk, 16, n, [wz.ap[:, k, 0:16] for k in range(KC)], [xnT[:, k, c0:c0 + n] for k in range(KC)],
                     reads=[wz.res] + xn_reads(H, c0, n))
            S.op(ACT, lambda: sca.copy(out=zT[0:16, c0:c0 + n], in_=banks[bk][0:16, 0:n]), reads=[bank_res[bk]], writes=[zT_res])

        def P_steps(h, st):
            steps = []
            hold = {}

            def f_load():
                hold["qk"] = wload([w_in[l, :, C_Q + h * 128:C_Q + (h + 1) * 128], w_in[l, :, C_K + h * 128:C_K + (h + 1) * 128]], KC)
                hold["v"] = wload([w_in[l, :, C_V + h * 256:C_V + (h + 1) * 256]], KC)
                hold["og"] = wload([w_in[l, :, C_OG + h * 256:C_OG + (h + 1) * 256]], KC)
            steps.append(f_load)
            s_gate, s_dec, s_v, s_og = [], [], [], []
            groups = []
            if H.blocks[0][0] == 0:
                groups.append([H.blocks[0]])
            realb = [b_ for b_ in H.blocks if b_[0] != 0]
            for i_ in range(0, len(realb), 4):
                groups.append(realb[i_:i_ + 4])
            for gi_, grp in enumerate(groups):
                def f(grp=grp, gi_=gi_):
                    rows = grp[0][2]
                    bi0 = grp[0][0]
                    ng = len(grp)
                    n = ng * 128
                    bk = psum()

                    def fn():
                        ins = None
                        for j, (bi, c0, r_) in enumerate(grp):
                            ins = tns.matmul(banks[bk][0:rows, j * 128:(j + 1) * 128], lhsT=zT[0:17, c0:c0 + rows],
                                             rhs=wgk_bf[0:17, l * 512 + h * 128: l * 512 + (h + 1) * 128], start=True, stop=True)
                        return ins
                    S.op(PE, fn, reads=[zT_res, wgkbf_res], writes=[bank_res[bk]], cost=0.06 * ng)
                    tg = tmpG[gi_ % 2]
                    S.op(ACT, lambda: sca.activation(out=tg[0:rows, 0:n], in_=banks[bk][0:rows, 0:n], func=AF.Exp, scale=-1.0),
                         reads=[bank_res[bk]], writes=[tgG_res[gi_ % 2]])
                    S.op(ACT, lambda: sca.activation(out=l_h[0:rows, bi0:bi0 + ng, :].rearrange("p a b -> p (a b)"), in_=tg[0:rows, 0:n], func=AF.Ln,
                                                     bias=ccol(O_ONE, 1, rows), scale=1.0),
                         reads=[tgG_res[gi_ % 2], cst_res], writes=[lh_res[b_[0]] for b_ in grp])
                s_gate.append(f)
            for gi_, grp in enumerate(groups):
                def f(grp=grp, gi_=gi_):
                    rows = grp[0][2]
                    ng = len(grp)
                    cfirst = grp[0][1]
                    n = (ng - 1) * 128 + rows
                    o1 = O_TRIE if grp[0][0] == 0 else O_TRIR
                    tri = cst[0:rows, o1:o1 + rows]
                    bk = psum()

                    def fn():
                        ins = None
                        for j, (bi, c0, r_) in enumerate(grp):
                            ins = tns.matmul(banks[bk][:, j * 128:j * 128 + rows], lhsT=l_h[0:rows, bi, :], rhs=tri, start=True, stop=True)
                        return ins
                    S.op(PE, fn, reads=[lh_res[b_[0]] for b_ in grp] + [cst_res], writes=[bank_res[bk]], cost=0.22 * ng)
                    ebr = [eb_res[b_[0]] for b_ in grp]
                    S.op(ACT, lambda: sca.activation(out=EbT[:, cfirst:cfirst + n], in_=banks[bk][:, 0:n], func=AF.Exp),
                         reads=[bank_res[bk]], writes=ebr)
                    S.op(ACT, lambda: sca.activation(out=ENbT[:, cfirst:cfirst + n], in_=banks[bk][:, 0:n], func=AF.Exp, scale=-1.0),
                         reads=[bank_res[bk]], writes=ebr)
                    for j, (bi, c0, r_) in enumerate(grp):
                        S.op(ACT, lambda: sca.copy(out=st.dec[:, 16 + bi:17 + bi], in_=EbT[:, c0 + rows - 1:c0 + rows]),
                             reads=ebr, writes=[st.dec_res])
                    if grp[0][0] == 0:
                        S.op(ACT, lambda: sca.copy(out=st.dec[:, 0:16], in_=EbT[:, 0:16]), reads=ebr, writes=[st.dec_res])
                s_dec.append(f)
            groups2 = []
            if H.blocks[0][0] == 0:
                groups2.append([H.blocks[0]])
            for i_ in range(0, len(realb), 2):
                groups2.append(realb[i_:i_ + 2])
            for gi_, grp in enumerate(groups2):
                def f(grp=grp, gi_=gi_):
                    wv = hold["v"]
                    rows = grp[0][2]
                    bi0 = grp[0][0]
                    ng = len(grp)
                    bk = psum()

                    def fn():
                        ins = None
                        for j, (bi, c0, r_) in enumerate(grp):
                            for k in range(KC):
                                ins = tns.matmul(banks[bk][0:rows, j * 256:(j + 1) * 256], lhsT=xnT[:, k, c0:c0 + rows], rhs=wv.ap[:, k, :],
                                                 start=(k == 0), stop=(k == KC - 1))
                        return ins
                    S.op(PE, fn, reads=[wv.res] + [xn_res[b_[0]] for b_ in grp], writes=[bank_res[bk]], cost=ng * KC * 256 / 2400.0)
                    S.op(ACT, lambda: sca.copy(out=st.v[0:rows, bi0:bi0 + ng, :].rearrange("p a b -> p (a b)"), in_=banks[bk][0:rows, 0:ng * 256]),
                         reads=[bank_res[bk]], writes=[st.v_res[b_[0]] for b_ in grp])
                s_v.append(f)
            for gi_, grp in enumerate(groups2):
                def f(grp=grp, gi_=gi_):
                    wog = hold["og"]
                    rows = grp[0][2]
                    bi0 = grp[0][0]
                    ng = len(grp)
                    n = ng * 256
                    bk = psum()

                    def fn():
                        ins = None
                        for j, (bi, c0, r_) in enumerate(grp):
                            for k in range(KC):
                                ins = tns.matmul(banks[bk][0:rows, j * 256:(j + 1) * 256], lhsT=xnT[:, k, c0:c0 + rows], rhs=wog.ap[:, k, :],
                                                 start=(k == 0), stop=(k == KC - 1))
                        return ins
                    S.op(PE, fn, reads=[wog.res] + [xn_res[b_[0]] for b_ in grp], writes=[bank_res[bk]], cost=ng * KC * 256 / 2400.0)
                    tg = tmpG[gi_ % 2]
                    tr = [tgG_res[gi_ % 2]]
                    S.op(ACT, lambda: sca.activation(out=tg[0:rows, 0:n], in_=banks[bk][0:rows, 0:n], func=AF.Exp, scale=-1.0),
                         reads=[bank_res[bk]], writes=tr)
                    S.op(ACT, lambda: sca.activation(out=tg[0:rows, 0:n], in_=tg[0:rows, 0:n], func=AF.Ln, bias=ccol(O_ONE, 1, rows), scale=1.0),
                         reads=tr + [cst_res], writes=tr)
                    S.op(ACT, lambda: sca.activation(out=tg[0:rows, 0:n], in_=tg[0:rows, 0:n], func=AF.Exp, scale=-1.0), reads=tr, writes=tr)
                    S.op(DVE, lambda: vec.tensor_tensor(out=tg[0:rows, 0:n], in0=banks[bk][0:rows, 0:n], in1=tg[0:rows, 0:n], op=ALU.mult),
                         reads=[bank_res[bk]] + tr, writes=tr)
                    gb_ = gain_bc[0:rows, l * 256:(l + 1) * 256].unsqueeze(1).broadcast_to([rows, ng, 256])
                    S.op(DVE, lambda: vec.tensor_tensor(out=st.gsog[0:rows, bi0:bi0 + ng, :], in0=tg[0:rows, 0:n].rearrange("p (a b) -> p a b", a=ng),
                                                        in1=gb_, op=ALU.mult),
                         reads=tr + [cst_res], writes=[st.g_res[b_[0]] for b_ in grp])
                s_og.append(f)
            s_kh = []
            s_qk = []
            for (c0, n) in H.tiles:
                def f(c0=c0, n=n):
                    wqk = hold["qk"]
                    blks = H.blk_of(c0, n)
                    bq = psum()
                    mm_group(bq, 128, n, [wqk.ap[:, k, 0:128] for k in range(KC)], [xnT[:, k, c0:c0 + n] for k in range(KC)],
                             reads=[wqk.res] + xn_reads(H, c0, n))
                    S.op(DVE, lambda: vec.scalar_tensor_tensor(out=st.qtil[:, c0:c0 + n], in0=banks[bq][:, 0:n], scalar=128.0 ** -0.5,
                                                               in1=EbT[:, c0:c0 + n], op0=ALU.mult, op1=ALU.mult),
                         reads=[bank_res[bq]] + [eb_res[b] for b in blks], writes=[st.q_res[b] for b in blks])
                    if isA and c0 == 0:
                        S.op(DVE, lambda: vec.tensor_scalar(out=st.q32s[:, 0:16], in0=banks[bq][:, 0:16], scalar1=128.0 ** -0.5, scalar2=None,
                                                            op0=ALU.mult),
                             reads=[bank_res[bq]], writes=[st.q32_res])
                def f2(c0=c0, n=n):
                    wqk = hold["qk"]
                    blks = H.blk_of(c0, n)
                    bkk = psum()
                    mm_group(bkk, 128, n, [wqk.ap[:, k, 128:256] for k in range(KC)], [xnT[:, k, c0:c0 + n] for k in range(KC)],
                             reads=[wqk.res] + xn_reads(H, c0, n))
                    S.op(DVE, lambda: vec.tensor_tensor(out=st.ktil[:, c0:c0 + n], in0=banks[bkk][:, 0:n], in1=ENbT[:, c0:c0 + n], op=ALU.mult),
                         reads=[bank_res[bkk]] + [eb_res[b] for b in blks], writes=[st.q_res[b] for b in blks])
                s_qk.append(f)
                s_qk.append(f2)
            for gi_, grp in enumerate(groups):
                def f(grp=grp, gi_=gi_):
                    rows = grp[0][2]
                    bi0 = grp[0][0]
                    ng = len(grp)
                    bk = psum()

                    def fn():
                        ins = None
                        for j, (bi, c0, r_) in enumerate(grp):
                            ins = tns.transpose(out=banks_bf[bk][0:rows, j * 128:(j + 1) * 128], in_=st.ktil[:, c0:c0 + rows], identity=ident_bf[:, :])
                        return ins
                    S.op(PE, fn, reads=[st.q_res[b_[0]] for b_ in grp] + [idbf_res], writes=[bank_res[bk]], cost=0.06 * ng)
                    S.op(ACT, lambda: sca.copy(out=st.khat[0:rows, bi0:bi0 + ng, :].rearrange("p a b -> p (a b)"), in_=banks_bf[bk][0:rows, 0:ng * 128]),
                         reads=[bank_res[bk]], writes=[st.kh_res[b_[0]] for b_ in grp])
                s_kh.append(f)
            steps.extend(s_gate)
            steps.extend(s_v)
            steps.extend(s_dec)
            steps.extend(s_qk)
            steps.extend(s_kh)
            steps.extend(s_og)
            return steps

        def R_steps(h, st):
            sres = S_res[l][h]
            Sap = Sst[:, l, h, :]
            state = {"sbf": None, "bo": {}, "i": 0, "pdec": None}
            blocks = H.blocks
            nb = len(blocks)

            def next_sbf():
                state["i"] ^= 1
                return state["i"]

            def s1(bi, c0, rows):
                ats = ATs[bi % 2]
                ba = psum()
                S.op(PE, lambda: tns.matmul(banks[ba][0:rows, 0:rows], lhsT=st.ktil[:, c0:c0 + rows], rhs=st.qtil[:, c0:c0 + rows], start=True, stop=True),
                     reads=[st.q_res[bi]], writes=[bank_res[ba]])
                mo = O_MCE if bi == 0 else O_MCR
                S.op(DVE, lambda: vec.tensor_tensor(out=ats[0:rows, 0:rows], in0=banks[ba][0:rows, 0:rows], in1=cst[0:rows, mo:mo + rows], op=ALU.mult),
                     reads=[bank_res[ba], cst_res], writes=[at_res[bi % 2]])
                if bi == 0:
                    S.op(DVE, lambda: vec.tensor_tensor(out=Qm[:, :, :], in0=st.q32s[:, 0:16].unsqueeze(2).broadcast_to([128, 16, 48]),
                                                        in1=cst[:, O_DELTA:O_DELTA + 768].rearrange("p (s j) -> p s j", s=16), op=ALU.mult),
                         reads=[st.q32_res, cst_res], writes=[Qm_res])
                    S.op(DVE, lambda: vec.tensor_tensor(out=KM[0:16, :, :], in0=st.khat[0:16, 0:1, :].broadcast_to([16, 16, 128]),
                                                        in1=cst[0:16, O_ID:O_ID + 16].unsqueeze(2).broadcast_to([16, 16, 128]), op=ALU.mult),
                         reads=[st.kh_res[0], cst_res], writes=[KM_res])

            def s2_real(bi, c0, rows, last):
                ats = ATs[bi % 2]
                if state["sbf"] is None:
                    i = next_sbf()
                    S.op(ACT, lambda: sca.copy(out=S_bf[i][:, :], in_=Sap), reads=[sres], writes=[Sbf_res[i]])
                    state["sbf"] = i
                cur = state["sbf"]
                bo = psum()
                state["bo"][bi] = bo

                def fn():
                    tns.matmul(banks[bo][0:rows, 0:256], lhsT=ats[0:rows, 0:rows], rhs=st.v[0:rows, bi, :], start=True, stop=False)
                    return tns.matmul(banks[bo][0:rows, 0:256], lhsT=st.qtil[:, c0:c0 + rows], rhs=S_bf[cur][:, :], start=False, stop=True)
                S.op(PE, fn, reads=[at_res[bi % 2], st.v_res[bi], st.q_res[bi], Sbf_res[cur]], writes=[bank_res[bo]])
                bu = psum()
                S.op(PE, lambda: tns.matmul(banks[bu][:, 0:256], lhsT=st.khat[0:rows, bi, :], rhs=st.v[0:rows, bi, :], start=True, stop=True),
                     reads=[st.kh_res[bi], st.v_res[bi]], writes=[bank_res[bu]])
                dcol = st.dec[:, 16 + bi:17 + bi]
                pd = state["pdec"]
                if pd is None:
                    S.op(DVE, lambda: vec.tensor_tensor(out=Sap, in0=banks[bu][:, 0:256], in1=Sap, op=ALU.add),
                         reads=[sres, bank_res[bu]], writes=[sres])
                else:
                    S.op(DVE, lambda: vec.scalar_tensor_tensor(out=Sap, in0=Sap, scalar=pd, in1=banks[bu][:, 0:256], op0=ALU.mult, op1=ALU.add),
                         reads=[sres, bank_res[bu], st.dec_res], writes=[sres])
                if not last:
                    i = next_sbf()
                    S.op(DVE, lambda: vec.tensor_scalar(out=S_bf[i][:, :], in0=Sap, scalar1=dcol, scalar2=None, op0=ALU.mult),
                         reads=[sres, st.dec_res], writes=[Sbf_res[i]])
                    state["sbf"] = i
                    state["pdec"] = dcol
                else:
                    S.op(DVE, lambda: vec.tensor_scalar(out=Sap, in0=Sap, scalar1=dcol, scalar2=None, op0=ALU.mult),
                         reads=[sres, st.dec_res], writes=[sres])
                    state["pdec"] = None

            def s2_E_begin(bi, c0, rows):
                ats = ATs[bi % 2]
                bo = 7
                state["bo"][bi] = bo
                S.op(PE, lambda: tns.matmul(banks[bo][0:48, 0:256], lhsT=ats[32:48, 0:48], rhs=st.v[32:48, 0, :], start=True, stop=False),
                     reads=[at_res[bi % 2], st.v_res[0]], writes=[bank_res[bo]])

            def s2_E_pair_a(p):
                sb = p % 2
                S.dma(SP, Ssamp_res[sb], [(Ssamp[sb][:, :, :], state_gla[l, 2 * p:2 * p + 2, h, :, :].rearrange("s d v -> d s v"))],
                      writes=[Ssamp_res[sb]])
                bu = psum()

                def fn_u():
                    ins = None
                    for i in range(2):
                        ins = tns.matmul(banks[bu][:, i * 256:(i + 1) * 256], lhsT=KM[0:16, 2 * p + i, :], rhs=st.v[0:16, 0, :], start=True, stop=True)
                    return ins
                S.op(PE, fn_u, reads=[KM_res, st.v_res[0]], writes=[bank_res[bu]])
                S.op(DVE, lambda: vec.tensor_tensor(out=Ssamp[sb][:, :, :].rearrange("p s v -> p (s v)"), in0=banks[bu][:, 0:512],
                                                    in1=Ssamp[sb][:, :, :].rearrange("p s v -> p (s v)"), op=ALU.add),
                     reads=[Ssamp_res[sb], bank_res[bu]], writes=[Ssamp_res[sb]])
                for i in range(2):
                    s_ = 2 * p + i
                    S.op(DVE, lambda: vec.tensor_scalar(out=Ssamp_bf[sb][:, i, :], in0=Ssamp[sb][:, i, :], scalar1=st.dec[:, s_:s_ + 1], scalar2=None, op0=ALU.mult),
                         reads=[Ssamp_res[sb], st.dec_res], writes=[Ssbf_res[sb]])
                    S.op(DVE, lambda: vec.tensor_scalar(out=Ssamp[sb][:, i, :], in0=Ssamp[sb][:, i, :], scalar1=st.dec[:, s_:s_ + 1], scalar2=None, op0=ALU.mult),
                         reads=[Ssamp_res[sb], st.dec_res], writes=[Ssamp_res[sb]])
                S.dma(ACT, Ssamp_res[sb], [(s_gla[l, 2 * p:2 * p + 2, h, :, :].rearrange("s d v -> d s v"), Ssamp[sb][:, :, :])],
                      reads=[Ssamp_res[sb]], is_output=True)

            def s2_E_pair_b(p):
                bo = 7
                sb = p % 2

                def fn():
                    ins = None
                    for i in range(2):
                        s_ = 2 * p + i
                        ins = tns.matmul(banks[bo][0:48, 0:256], lhsT=Qm[:, s_, :], rhs=Ssamp_bf[sb][:, i, :], start=False, stop=(s_ == 15))
                    return ins
                S.op(PE, fn, reads=[Qm_res, Ssbf_res[sb]], writes=[bank_res[bo]])

            def s2_E_end(bi, c0, rows):
                bu = psum()
                S.op(PE, lambda: tns.matmul(banks[bu][:, 0:256], lhsT=st.khat[32:48, 0, :], rhs=st.v[32:48, 0, :], start=True, stop=True),
                     reads=[st.kh_res[0], st.v_res[0]], writes=[bank_res[bu]])
                i = next_sbf()
                dE = st.dec[:, 16:17]
                S.op(DVE, lambda: vec.tensor_scalar(out=S_bf[i][:, :], in0=banks[bu][:, 0:256], scalar1=dE, scalar2=None, op0=ALU.mult),
                     reads=[bank_res[bu], st.dec_res], writes=[Sbf_res[i]])
                state["sbf"] = i
                S.op(DVE, lambda: vec.tensor_scalar(out=Sap, in0=banks[bu][:, 0:256], scalar1=dE, scalar2=None, op0=ALU.mult),
                     reads=[bank_res[bu], st.dec_res], writes=[sres])

            def s3(bi, c0, rows):
                bo = state["bo"][bi]
                S.op(ACT, lambda: sca.activation(out=sqj[0:rows, 0:256], in_=banks[bo][0:rows, 0:256], func=AF.Square,
                                                 accum_out=stat[0:rows, 40:41]),
                     reads=[bank_res[bo]], writes=[sqj_res, stat_res[4]])
                S.op(ACT, lambda: sca.activation(out=stat[0:rows, 42:43], in_=stat[0:rows, 40:41], func=AF.Ln, scale=1.0 / 256,
                                                 bias=ccol(O_EPS, 1, rows)),
                     reads=[stat_res[4], cst_res], writes=[stat_res[6]])
                S.op(ACT, lambda: sca.activation(out=stat[0:rows, 43:44], in_=stat[0:rows, 42:43], func=AF.Exp, scale=-0.5),
                     reads=[stat_res[6]], writes=[stat_res[7]])
                pb = pb_tm[bi % 2]
                S.op(DVE, lambda: vec.scalar_tensor_tensor(out=pb[0:rows, :], in0=banks[bo][0:rows, 0:256], scalar=stat[0:rows, 43:44],
                                                           in1=st.gsog[0:rows, bi, :], op0=ALU.mult, op1=ALU.mult),
                     reads=[bank_res[bo], stat_res[7], st.g_res[bi]], writes=[pb_res[bi % 2]])

            def s4(bi, c0, rows):
                pb = pb_tm[bi % 2]
                bt = psum()

                def fn():
                    ins = None
                    for i in range(2):
                        ins = tns.transpose(out=banks_bf[bt][:, i * 128:i * 128 + rows], in_=pb[0:rows, i * 128:(i + 1) * 128],
                                            identity=ident_bf[0:rows, 0:rows])
                    return ins
                S.op(PE, fn, reads=[pb_res[bi % 2], idbf_res], writes=[bank_res[bt]])
                src = banks_bf[bt][:, 0:256].rearrange("p (k c) -> p k c", k=2)[:, :, 0:rows]
                S.op(DVE, lambda: vec.tensor_copy(out=Rt[:, 2 * h:2 * h + 2, c0:c0 + rows], in_=src),
                     reads=[bank_res[bt]], writes=[slab_res[2 * h], slab_res[2 * h + 1]])

            LM, LS, LE, LP = 2.5, 3.0, 8.0, 4.0
            steps = []
            real = [b for b in blocks if b[0] != 0]
            nr = len(real)
            hasE = blocks[0][0] == 0
            steps.append(("s1_%d" % blocks[0][0], lambda: s1(*blocks[0]), []))
            if hasE:
                steps.append(("s1_%d" % real[0][0], lambda b=real[0]: s1(*b), []))
                steps.append(("Eb", lambda b=blocks[0]: s2_E_begin(*b), [("s1_0", LM)]))
                steps.append(("s2_0", lambda b=blocks[0]: s2_E_end(*b), []))
            for idx, (bi, c0, rows) in enumerate(real):
                if idx + 1 < nr:
                    steps.append(("s1_%d" % real[idx + 1][0], lambda b=real[idx + 1]: s1(*b), []))
                if isA:
                    steps.append(("a%d" % idx, lambda p=idx: s2_E_pair_a(p), [("a%d" % (idx - 2), 6.0)] if idx >= 2 else []))
                prev = ("s2_%d" % real[idx - 1][0]) if idx >= 1 else ("s2_0" if hasE else None)
                deps = [("s1_%d" % bi, LM)] + ([(prev, LS)] if prev else [])

                def f2(b=real[idx], last=(idx == nr - 1)):
                    s2_real(b[0], b[1], b[2], last)
                    s3(*b)
                steps.append(("s2_%d" % bi, f2, deps))
                if isA and idx >= 1:
                    steps.append(("b%d" % (idx - 1), lambda p=idx: s2_E_pair_b(p - 1), [("a%d" % (idx - 1), LP)]))
                if idx >= 1:
                    steps.append(("s4_%d" % real[idx - 1][0], lambda b=real[idx - 1]: s4(*b), [("s2_%d" % real[idx - 1][0], LE)]))
            steps.append(("s4_%d" % real[nr - 1][0], lambda b=real[nr - 1]: s4(*b), [("s2_%d" % real[nr - 1][0], LE)]))
            if isA:
                def fE():
                    s2_E_pair_b(7)
                    s3(*blocks[0])
                steps.append(("b7", fE, [("a7", LP)]))
                steps.append(("s4_0", lambda b=blocks[0]: s4(*b), [("b7", LE)]))
            if H.name == "B":
                steps.append(("pgla", lambda: S.dma(ACT, sres, [(p_gla[l, h, :, :], Sap)], reads=[sres], is_output=True), []))
            return steps

        def merge(rs, ps):
            issued = {}
            pi = 0
            for (name, fn, deps) in rs:
                t_ready = max([issued.get(d, -1e9) + lat for d, lat in deps] + [-1e9])
                while S.pe_time < t_ready and pi < len(ps):
                    ps[pi]()
                    pi += 1
                issued[name] = S.pe_time
                fn()
            while pi < len(ps):
                ps[pi]()
                pi += 1

        for f in P_steps(0, gsets[0]):
            f()
        for h in range(4):
            rs = R_steps(h, gsets[h % 2])
            if h < 3:
                ps = P_steps(h + 1, gsets[(h + 1) % 2])
            else:
                dead = lh_res + eb_res + ec_res + tgG_res + gsets[0].q_res + gsets[0].kh_res + gsets[0].v_res
                alias_end(dead, slab_res[8:22])
                ps = gate_steps(H, l, C_GBB)
            merge(rs, ps)
        alias_end(newres, oldres)

    def gate_steps(H, l, gcol):
        steps = []
        hold = {}
        for jp in range(4):
            def fl(jp=jp):
                hold[jp] = wload([w_in[l, :, gcol + jp * 256:gcol + (jp + 1) * 256]], KC)
            steps.append(fl)
            for jj in range(2):
                for (c0, n) in H.tiles:
                    def f(jp=jp, jj=jj, c0=c0, n=n):
                        w = hold[jp]
                        jo = 2 * jp + jj
                        bg = psum()
                        mm_group(bg, 128, n, [w.ap[:, k, jj * 128:(jj + 1) * 128] for k in range(KC)], [xnT[:, k, c0:c0 + n] for k in range(KC)],
                                 reads=[w.res] + xn_reads(H, c0, n))
                        tb = sgb[(jo + c0 // 512) % 2]
                        tres = cells_of(tb)
                        S.op(ACT, lambda: sca.activation(out=tb[:, 0:n], in_=banks[bg][:, 0:n], func=AF.Exp, scale=-1.0),
                             reads=[bank_res[bg]], writes=tres)
                        S.op(ACT, lambda: sca.activation(out=tb[:, 0:n], in_=tb[:, 0:n], func=AF.Ln, bias=ccol(O_ONE), scale=1.0),
                             reads=tres + [cst_res], writes=tres)
                        S.op(ACT, lambda: sca.activation(out=Rt[:, 8 + jo, c0:c0 + n], in_=tb[:, 0:n], func=AF.Exp, scale=-1.0),
                             reads=tres, writes=[slab_res[8 + jo]])
                    steps.append(f)
        return steps

    def yb_phase(H, l):
        for jo in range(8):
            w = wload([w_b[l, :, jo * 128:(jo + 1) * 128]], KC)
            for (c0, n) in H.tiles:
                by = psum()
                mm_group(by, 128, n, [w.ap[:, k, 0:128] for k in range(KC)], [Rt[:, k, c0:c0 + n] for k in range(KC)],
                         reads=[w.res] + [slab_res[k] for k in range(KC)])
                S.op(DVE, lambda: vec.tensor_tensor(out=Rt[:, 8 + jo, c0:c0 + n], in0=banks[by][:, 0:n], in1=Rt[:, 8 + jo, c0:c0 + n], op=ALU.mult),
                     reads=[bank_res[by], slab_res[8 + jo]], writes=[slab_res[8 + jo]])

    def gate_out_phase(H, l, wmat, gcol, final):
        tbuf = sgb
        tres_all = [cells_of(t) for t in tbuf]
        cnt = 0
        for jo in range(8):
            w = wload([wmat[l, :, jo * 128:(jo + 1) * 128], w_in[l, :, gcol + jo * 128:gcol + (jo + 1) * 128]], KC)
            for (c0, n) in H.tiles:
                by = psum()
                mm_group(by, 128, n, [w.ap[:, k, 0:128] for k in range(KC)], [Rt[:, k, c0:c0 + n] for k in range(KC)],
                         reads=[w.res] + [slab_res[k] for k in range(KC)])
                bg = psum()
                mm_group(bg, 128, n, [w.ap[:, k, 128:256] for k in range(KC)], [xnT[:, k, c0:c0 + n] for k in range(KC)],
                         reads=[w.res] + xn_reads(H, c0, n))
                tb = tbuf[cnt % 2]
                tres = tres_all[cnt % 2]
                cnt += 1
                S.op(ACT, lambda: sca.activation(out=tb[:, 0:n], in_=banks[bg][:, 0:n], func=AF.Sigmoid), reads=[bank_res[bg]], writes=tres)
                if not final:
                    S.op(DVE, lambda: vec.tensor_tensor(out=Rt[:, 8 + jo, c0:c0 + n], in0=banks[by][:, 0:n], in1=tb[:, 0:n], op=ALU.mult),
                         reads=[bank_res[by]] + tres, writes=[slab_res[8 + jo]])
                else:
                    S.op(DVE, lambda: vec.tensor_tensor(out=tb[:, 0:n], in0=banks[by][:, 0:n], in1=tb[:, 0:n], op=ALU.mult),
                         reads=[bank_res[by]] + tres, writes=tres)
                    S.op(DVE, lambda: vec.tensor_tensor(out=Rt[:, 8 + jo, c0:c0 + n], in0=tb[:, 0:n], in1=Rt[:, 8 + jo, c0:c0 + n], op=ALU.add),
                         reads=tres + [slab_res[8 + jo]], writes=[slab_res[8 + jo]])

    def conv_phase(H, l):
        isA = H.name == "A"
        ures = cells_of(U)
        if isA:
            scres = cells_of(sc_tm)
            pres = cells_of(prevT)
            S.dma(SP, scres[0], [(sc_tm[0:16, :, :], state_conv[l, :, :, :])], writes=scres)
            bk = psum()

            def fn():
                ins = None
                for r in range(2):
                    for j in range(8):
                        ins = tns.transpose(out=banks[bk][:, (r * 8 + j) * 16:(r * 8 + j + 1) * 16], in_=sc_tm[0:16, r, j * 128:(j + 1) * 128],
                                            identity=cst[0:16, O_ID:O_ID + 16])
                return ins
            S.op(PE, fn, reads=scres + [cst_res], writes=[bank_res[bk]])
            S.op(ACT, lambda: sca.copy(out=prevT[:, :, :].rearrange("p a s -> p (a s)"), in_=banks[bk][:, 0:256]), reads=[bank_res[bk]], writes=pres)
            dres = Res("sconv_copy%d" % l)
            S.dma(SP, dres, [(s_conv[l, :, 0, :], state_conv[l, :, 1, :])], is_output=True)
        yi = 0
        pend_t = []
        for j in range(8):
            w = wload([w_in[l, :, C_GC + j * 128:C_GC + (j + 1) * 128], w_in[l, :, C_H + j * 128:C_H + (j + 1) * 128],
                       w_in[l, :, C_GB + j * 128:C_GB + (j + 1) * 128]], KC)
            cw = lambda r: cwT[:, l * 24 + j * 3 + r:l * 24 + j * 3 + r + 1]
            if isA:
                S.op(DVE, lambda: vec.memset(U[:, 0:2], 0.0), writes=ures)
            else:
                S.op(ACT, lambda: sca.copy(out=U[:, 0:2], in_=carry[:, l, j, :]), reads=[carry_res[l][j]], writes=ures)
            for (c0, n) in H.tiles:
                bgc, bh, bgb = psum(), psum(), psum()
                xr = xn_reads(H, c0, n)
                rhs = [xnT[:, k, c0:c0 + n] for k in range(KC)]
                mm_group(bgc, 128, n, [w.ap[:, k, 0:128] for k in range(KC)], rhs, reads=[w.res] + xr)
                mm_group(bh, 128, n, [w.ap[:, k, 128:256] for k in range(KC)], rhs, reads=[w.res] + xr)
                mm_group(bgb, 128, n, [w.ap[:, k, 256:384] for k in range(KC)], rhs, reads=[w.res] + xr)
                while pend_t:
                    pend_t.pop(0)()
                gc = gcb[yi % 2]
                gcres = cells_of(gc)
                Y = Ybuf[yi % 2]
                yres = cells_of(Y)
                yi += 1
                S.op(ACT, lambda: sca.copy(out=gc[:, 0:n], in_=banks[bgc][:, 0:n]), reads=[bank_res[bgc]], writes=gcres)
                S.op(DVE, lambda: vec.tensor_tensor(out=U[:, 2 + c0:2 + c0 + n], in0=banks[bh][:, 0:n], in1=gc[:, 0:n], op=ALU.mult),
                     reads=[bank_res[bh]] + gcres, writes=ures)
                S.op(ACT, lambda: sca.activation(out=Y[:, 0:n], in_=U[:, c0:c0 + n], func=AF.Copy, scale=cw(0)), reads=ures + [cst_res], writes=yres)
                S.op(DVE, lambda: vec.scalar_tensor_tensor(out=Y[:, 0:n], in0=U[:, c0 + 1:c0 + 1 + n], scalar=cw(1), in1=Y[:, 0:n],
                                                           op0=ALU.mult, op1=ALU.add), reads=ures + yres + [cst_res], writes=yres)
                S.op(DVE, lambda: vec.scalar_tensor_tensor(out=Y[:, 0:n], in0=U[:, c0 + 2:c0 + 2 + n], scalar=cw(2), in1=Y[:, 0:n],
                                                           op0=ALU.mult, op1=ALU.add), reads=ures + yres + [cst_res], writes=yres)
                S.op(DVE, lambda: vec.tensor_tensor(out=Rt[:, j, c0:c0 + n], in0=banks[bgb][:, 0:n], in1=Y[:, 0:n], op=ALU.mult),
                     reads=[bank_res[bgb]] + yres, writes=[slab_res[j]])
                if isA and c0 == 0:
                    ysr = cells_of(Ys)
                    S.op(DVE, lambda: vec.tensor_scalar(out=Ys[:, :], in0=prevT[:, j, :], scalar1=cw(0), scalar2=None, op0=ALU.mult),
                         reads=pres + [cst_res], writes=ysr)
                    S.op(DVE, lambda: vec.scalar_tensor_tensor(out=Ys[:, :], in0=prevT[:, 8 + j, :], scalar=cw(1), in1=Ys[:, :],
                                                               op0=ALU.mult, op1=ALU.add), reads=pres + ysr + [cst_res], writes=ysr)
                    S.op(DVE, lambda: vec.scalar_tensor_tensor(out=Ys[:, :], in0=U[:, 2:18], scalar=cw(2), in1=Ys[:, :],
                                                               op0=ALU.mult, op1=ALU.add), reads=ures + ysr + [cst_res], writes=ysr)
                    S.op(DVE, lambda: vec.tensor_tensor(out=Rt[:, j, 0:16], in0=banks[bgb][:, 0:16], in1=Ys[:, :], op=ALU.mult),
                         reads=[bank_res[bgb]] + ysr, writes=[slab_res[j]])
            nc_ = H.ncols
            if isA:
                S.op(ACT, lambda: sca.copy(out=carry[:, l, j, :], in_=U[:, nc_:nc_ + 2]), reads=ures, writes=[carry_res[l][j]])
                def f_t(j=j):
                    bt = psum()
                    S.op(PE, lambda: tns.transpose(out=banks[bt][0:16, 0:128], in_=U[:, 2:18], identity=cst[:, O_ID:O_ID + 128]),
                         reads=ures + [cst_res], writes=[bank_res[bt]])
                    S.op(ACT, lambda: sca.copy(out=us_tm[0:16, 1, j * 128:(j + 1) * 128], in_=banks[bt][0:16, 0:128]),
                         reads=[bank_res[bt]], writes=scres)
                pend_t.append(f_t)
            else:
                def f_t(j=j, nc_=nc_):
                    bt = psum()
                    S.op(PE, lambda: tns.transpose(out=banks[bt][0:2, 0:128], in_=U[:, nc_:nc_ + 2], identity=cst[:, O_ID:O_ID + 128]),
                         reads=ures + [cst_res], writes=[bank_res[bt]])
                    S.op(ACT, lambda: sca.copy(out=pc_tm[0:2, j * 128:(j + 1) * 128], in_=banks[bt][0:2, 0:128]),
                         reads=[bank_res[bt]], writes=cells_of(pc_tm))
                pend_t.append(f_t)
        while pend_t:
            pend_t.pop(0)()
        if isA:
            S.dma(ACT, scres[0], [(s_conv[l, :, 1, :], us_tm[0:16, 1, :])], reads=scres, is_output=True)
        else:
            pcr = cells_of(pc_tm)
            S.dma(ACT, pcr[0], [(p_conv[l, :, :], pc_tm[0:2, :])], reads=pcr, is_output=True)

    def tm_out_phase(H, wmat_l, nk, src_slab0, npipe=None):
        npiece = (nk + 10) // 11 if nk > 8 else 1
        kper = nk // npiece

        def one(ws, q, bi, c0, rows):
            bk = psum()
            lhs = [Rt[:, src_slab0 + k, c0:c0 + rows] for k in range(nk)]
            rhs = [ws[k // kper].ap[:, k % kper, :] for k in range(nk)]
            mm_group(bk, rows, 256, lhs, rhs, reads=[w.res for w in ws] + [slab_res[src_slab0 + k] for k in range(nk)])
            S.op(DVE, lambda: vec.tensor_tensor(out=x_tm[0:rows, bi, q * 256:(q + 1) * 256], in0=banks[bk][0:rows, 0:256],
                                                in1=x_tm[0:rows, bi, q * 256:(q + 1) * 256], op=ALU.add),
                 reads=[bank_res[bk], x_res[bi]], writes=[x_res[bi]])
        if npiece == 1:
            wq = [[wload([wmat_l[0:nk * 128, q * 256:(q + 1) * 256]], nk)] for q in range(4)]
            for (bi, c0, rows) in H.blocks:
                for q in range(4):
                    one(wq[q], q, bi, c0, rows)
                if npipe is not None:
                    npipe.block(bi, c0, rows)
        else:
            for q in range(4):
                ws = [wload([wmat_l[(pp * kper) * 128:((pp + 1) * kper) * 128, q * 256:(q + 1) * 256]], kper) for pp in range(npiece)]
                for (bi, c0, rows) in H.blocks:
                    one(ws, q, bi, c0, rows)
                    if q == 3 and npipe is not None:
                        npipe.block(bi, c0, rows)
        if npipe is not None:
            npipe.flush()

    def ffn_up_phase(H, l):
        sres_all = [cells_of(t) for t in sgf]
        cnt = 0
        for j in range(NJ):
            w = wload([w_gu[l, :, j * 128:(j + 1) * 128], w_gu[l, :, DFF + j * 128:DFF + (j + 1) * 128]], KC)
            for (c0, n) in H.tiles:
                bg, bu = psum(), psum()
                xr = xn_reads(H, c0, n)
                rhs = [xnT[:, k, c0:c0 + n] for k in range(KC)]
                mm_group(bg, 128, n, [w.ap[:, k, 0:128] for k in range(KC)], rhs, reads=[w.res] + xr)
                mm_group(bu, 128, n, [w.ap[:, k, 128:256] for k in range(KC)], rhs, reads=[w.res] + xr)
                tb = sgf[cnt % 2]
                tres = sres_all[cnt % 2]
                cnt += 1
                S.op(ACT, lambda: sca.activation(out=tb[:, 0:n], in_=banks[bg][:, 0:n], func=AF.Silu), reads=[bank_res[bg]], writes=tres)
                S.op(DVE, lambda: vec.tensor_tensor(out=Rt[:, j, c0:c0 + n], in0=banks[bu][:, 0:n], in1=tb[:, 0:n], op=ALU.mult),
                     reads=[bank_res[bu]] + tres, writes=[slab_res[j]])

    for hn in halves:
        H = Half(hn)
        load_x(H)
        norm_phase(H, norm_mix[0:1, :], 0)
        for l in range(depth):
            gla_phase(H, l)
            yb_phase(H, l)
            conv_phase(H, l)
            gate_out_phase(H, l, w_a, C_GA, final=True)
            tm_out_phase(H, w_o[l], 8, 8, NormPipe(H, norm_ffn[l:l + 1, :], wj=2 + l))
            ffn_up_phase(H, l)
            if l + 1 < depth:
                nxt = NormPipe(H, norm_mix[l + 1:l + 2, :], wj=l + 1)
            else:
                nxt = NormPipe(H, final_norm[0:1, :], final=True)
            tm_out_phase(H, w_down[l], NJ, 0, nxt)
    S.finish()
    es.close()
    return nc


_NC_CACHE = {}


def kernel(x_prompt, x_sample, state_conv, state_gla, meta_tokens, w_in, conv_w, w_gk2, b_gk2,
           gla_gain, w_a_out, w_b_out, w_o, norm_mix, norm_ffn, w_gu, w_down, final_norm):
    f = lambda a: np.ascontiguousarray(np.asarray(a, dtype=np.float32))
    x_prompt, x_sample, state_conv, state_gla = f(x_prompt), f(x_sample), f(state_conv), f(state_gla)
    conv_wt = f(f(conv_w).reshape(2, 3, 8, 128).transpose(3, 0, 2, 1).reshape(128, 48))
    wgk_aug = f(np.concatenate([f(w_gk2).transpose(1, 0, 2).reshape(16, 1024), f(b_gk2).reshape(1, 1024)], axis=0))
    shared = {
        "meta_tokens": f(meta_tokens), "w_in": f(w_in), "conv_wt": conv_wt, "wgk_aug": wgk_aug, "gla_gain": f(gla_gain).reshape(1, 512),
        "w_a_out": f(w_a_out), "w_b_out": f(w_b_out), "w_o": f(w_o), "norm_mix": f(norm_mix), "norm_ffn": f(norm_ffn),
        "w_gu": f(w_gu), "w_down": f(w_down), "final_norm": f(final_norm).reshape(1, D), "consts": make_consts(),
        "normw_t": f(np.concatenate([f(norm_mix), f(norm_ffn)], axis=0).reshape(4, 8, 128).transpose(2, 0, 1).reshape(128, 32)),
    }
    in_maps = []
    for c in range(NCORE):
        m = dict(shared)
        m["x_prompt"] = f(x_prompt[c])
        m["x_sample"] = f(x_sample[16 * c:16 * (c + 1), 0, :])
        m["state_conv"] = f(state_conv[:, 16 * c:16 * (c + 1)])
        m["state_gla"] = f(state_gla[:, 16 * c:16 * (c + 1)])
        in_maps.append(m)
    if "nc" not in _NC_CACHE:
        _NC_CACHE["nc"] = build_nc()
    nc = _NC_CACHE["nc"]
    res = run_bass_kernel_spmd(nc, in_maps, core_ids=list(range(NCORE)))
    R = res.results
    y_prompt = np.stack([R[c]["y_prompt"] for c in range(NCORE)], axis=0)
    y_sample = np.concatenate([R[c]["y_sample"] for c in range(NCORE)], axis=0).reshape(128, 1, D)
    p_conv = np.stack([R[c]["p_conv"] for c in range(NCORE)], axis=1)
    p_gla = np.stack([R[c]["p_gla"] for c in range(NCORE)], axis=1)
    s_conv = np.concatenate([R[c]["s_conv"] for c in range(NCORE)], axis=1)
    s_gla = np.concatenate([R[c]["s_gla"] for c in range(NCORE)], axis=1)
    return (y_prompt.astype(np.float32), y_sample.astype(np.float32), p_conv.astype(np.float32),
            p_gla.astype(np.float32), s_conv.astype(np.float32), s_gla.astype(np.float32))
```

```python
import numpy as np
import concourse.bass as bass
import concourse.mybir as mybir
from concourse.bass_utils import run_bass_kernel_spmd
from contextlib import ExitStack

F32 = mybir.dt.float32
BF16 = mybir.dt.bfloat16
AF = mybir.ActivationFunctionType
ALU = mybir.AluOpType

D = 1024
KC = 8
DFF = 2816
NJ = 22
EPS = 1e-6
NCA = 1072
NCORE = 8
C_GB, C_GC, C_H, C_Q, C_K, C_V, C_OG, C_Z, C_GA, C_GBB = 0, 1024, 2048, 3072, 3584, 4096, 5120, 6144, 6160, 7184
WSLOT = 3072

O_ID, O_TRIR, O_TUR, O_TRIE, O_TUE, O_MCR, O_MCE, O_DELTA = 0, 128, 256, 384, 512, 640, 768, 896
O_EPS = O_DELTA + 16 * 48
O_NH = O_EPS + 1
O_ONE = O_NH + 1
NCF = O_ONE + 1


def make_consts():
    c = np.zeros((128, NCF), np.float32)
    idx = np.arange(128)
    c[:, O_ID:O_ID + 128] = np.eye(128, dtype=np.float32)
    s = idx[:, None]
    t = idx[None, :]
    g = -1.0 / 16.0
    c[:, O_TRIR:O_TRIR + 128] = np.where(s <= t, g, 0.0)
    c[:, O_TUR:O_TUR + 128] = np.where(s > t, g, 0.0)
    trie = np.zeros((128, 128), np.float32)
    tue = np.zeros((128, 128), np.float32)
    mce = np.zeros((128, 128), np.float32)
    for i in range(16):
        trie[i, i] = g
    for a in range(32, 48):
        for b in range(32, 48):
            if a <= b:
                trie[a, b] = g
                mce[a, b] = 1.0
            if a > b:
                tue[a, b] = g
    c[:, O_TRIE:O_TRIE + 128] = trie
    c[:, O_TUE:O_TUE + 128] = tue
    c[:, O_MCR:O_MCR + 128] = np.where(s <= t, 1.0, 0.0)
    c[:, O_MCE:O_MCE + 128] = mce
    dl = np.zeros((16, 48), np.float32)
    for i in range(16):
        dl[i, i] = 1.0
    c[:, O_DELTA:O_DELTA + 768] = dl.reshape(1, 768)
    c[:, O_EPS] = EPS
    c[:, O_NH] = -0.5
    c[:, O_ONE] = 1.0
    return c


class Res:
    __slots__ = ("name", "w", "r", "dsem", "dcnt", "dkey", "strict")

    def __init__(self, name):
        self.name = name
        self.w = None
        self.r = {}
        self.dsem = None
        self.dcnt = 0
        self.dkey = None
        self.strict = False


class Sched:
    def __init__(self, nc, es):
        self.nc = nc
        self.es = es
        self.eng = {"pe": nc.tensor, "act": nc.scalar, "dve": nc.vector, "pool": nc.gpsimd, "sp": nc.sync}
        self.sem = {k: es.enter_context(nc.semaphore("s_" + k)) for k in self.eng}
        self.cnt = {k: 0 for k in self.eng}
        self.seen = {k: {} for k in self.eng}
        self.semobj = dict(self.sem)
        self.out_events = []
        self.nwait = 0
        self.pe_time = 0.0
        self.strict_all = False

    def _deps(self, e, reads, writes):
        deps = {}

        def add(ev, own_ok):
            k, v = ev
            if k == e and not own_ok and not self.strict_all:
                return
            if k == e and e in ("pe", "sp"):
                return
            if deps.get(k, 0) < v:
                deps[k] = v

        for r in reads:
            if r.w is not None:
                add(r.w, True)
        for w in writes:
            if w.w is not None:
                add(w.w, w.strict)
            for k, v in w.r.items():
                add((k, v), False)
        return deps

    def _wait(self, e, deps):
        for k, v in deps.items():
            if self.seen[e].get(k, 0) >= v:
                continue
            self.eng[e].wait_ge(self.semobj[k], v)
            self.seen[e][k] = v
            self.nwait += 1

    def _mark(self, ev, reads, writes):
        k, v = ev
        for r in reads:
            if r.r.get(k, 0) < v:
                r.r[k] = v
        for w in writes:
            w.w = ev
            w.r = {}

    def op(self, e, fn, reads=(), writes=(), cost=None):
        self._wait(e, self._deps(e, reads, writes))
        if e == "pe":
            self.pe_time += 0.06 if cost is None else cost
        ins = fn()
        self.cnt[e] += 1
        ins.then_inc(self.sem[e], 1)
        self._mark((e, self.cnt[e]), reads, writes)

    def dma(self, q, res, pairs, reads=(), writes=(), is_output=False, **kw):
        self._wait(q, self._deps(q, reads, writes))
        key = res.dkey if res.dkey is not None else ("d", res.name)
        if res.dsem is None:
            res.dsem = self.es.enter_context(self.nc.semaphore("d_" + res.name))
        self.semobj[key] = res.dsem
        for (o, i) in pairs:
            self.eng[q].dma_start(out=o, in_=i, **kw).then_inc(res.dsem, 16)
            res.dcnt += 16
        ev = (key, res.dcnt)
        self._mark(ev, reads, writes)
        if is_output:
            self.out_events.append(ev)

    def finish(self):
        deps = {}
        for k, v in self.out_events:
            if deps.get(k, 0) < v:
                deps[k] = v
        for e in ("pe", "act", "dve", "pool"):
            if self.cnt[e] > 0:
                deps[e] = self.cnt[e]
        self._wait("sp", deps)


class Half:
    def __init__(self, name):
        self.name = name
        if name == "A":
            self.ncols = NCA
            self.blocks = [(0, 0, 48)] + [(r, 48 + 128 * (r - 1), 128) for r in range(1, 9)]
            self.tiles = [(0, 512), (512, 512), (1024, 48)]
            self.tok0 = 0
        else:
            self.ncols = 1024
            self.blocks = [(r, 128 * (r - 1), 128) for r in range(1, 9)]
            self.tiles = [(0, 512), (512, 512)]
            self.tok0 = 1024

    def blk_of(self, c0, n):
        out = []
        for (bi, b0, rows) in self.blocks:
            if b0 < c0 + n and c0 < b0 + rows:
                out.append(bi)
        return out


def build_nc(depth=2, halves=("A", "B"), dbg=None):
    nc = bass.Bass("TRN2", target_bir_lowering=False)

    def din(name, shape):
        return nc.dram_tensor(name, shape, F32, kind="ExternalInput").ap()

    def dout(name, shape):
        return nc.dram_tensor(name, shape, F32, kind="ExternalOutput").ap()

    x_prompt = din("x_prompt", [2048, D])
    x_sample = din("x_sample", [16, D])
    state_conv = din("state_conv", [2, 16, 2, D])
    state_gla = din("state_gla", [2, 16, 4, 128, 256])
    meta = din("meta_tokens", [16, D])
    w_in = din("w_in", [2, D, 8208])
    conv_wt = din("conv_wt", [128, 48])
    wgk_aug = din("wgk_aug", [17, 1024])
    gla_gain = din("gla_gain", [1, 512])
    w_a = din("w_a_out", [2, D, D])
    w_b = din("w_b_out", [2, D, D])
    w_o = din("w_o", [2, D, D])
    norm_mix = din("norm_mix", [2, D])
    norm_ffn = din("norm_ffn", [2, D])
    w_gu = din("w_gu", [2, D, 2 * DFF])
    w_down = din("w_down", [2, DFF, D])
    final_norm = din("final_norm", [1, D])
    cst_d = din("consts", [128, NCF])
    normw_d = din("normw_t", [128, 32])

    y_prompt = dout("y_prompt", [2048, D])
    y_sample = dout("y_sample", [16, D])
    p_conv = dout("p_conv", [2, 2, D])
    p_gla = dout("p_gla", [2, 4, 128, 256])
    s_conv = dout("s_conv", [2, 16, 2, D])
    s_gla = dout("s_gla", [2, 16, 4, 128, 256])
    dbg_out = {}
    if dbg:
        for name, shape in dbg.items():
            dbg_out[name] = dout("dbg_" + name, shape)

    es = ExitStack()
    S = Sched(nc, es)

    off = [16384]

    def alloc(name, shape, dt):
        nbytes = int(np.prod(shape[1:])) * (4 if dt == F32 else 2)
        o = (off[0] + 31) // 32 * 32
        t = nc.alloc_sbuf_tensor_at(name, list(shape), dt, offset=o)
        off[0] = o + nbytes
        return t

    def alloc_at(name, shape, dt, o):
        return nc.alloc_sbuf_tensor_at(name, list(shape), dt, offset=o)

    x_tm = alloc("x_tm", [128, 9, D], F32)
    xnT = alloc("xnT", [128, KC, NCA], BF16)
    R_off = (off[0] + 31) // 32 * 32
    SLAB = NCA * 2
    Rt = alloc("R", [128, NJ, NCA], BF16)
    NCELL, CELL = 30, 512
    wring = alloc("wring", [128, NCELL * CELL], BF16)
    gbc = [alloc("gbc%d" % i, [128, D], F32) for i in range(2)]
    xn_bf = [alloc("xn_bf%d" % i, [128, D], BF16) for i in range(3)]
    sqj = alloc("sqj", [128, D], BF16)
    Sst = alloc("Sst", [128, 2, 4, 256], F32)
    S_bf = [alloc("S_bf%d" % i, [128, 256], BF16) for i in range(2)]
    cst = alloc("cst", [128, NCF], F32)
    ident_bf = alloc("ident_bf", [128, 128], BF16)
    cwT = alloc("cwT", [128, 48], F32)
    wgk = alloc("wgk", [128, 1024], F32)
    gain_bc = alloc("gain_bc", [128, 512], F32)
    stat = alloc("stat", [128, 64], F32)
    normw = alloc("normw", [128, 32], F32)
    carry = alloc("carry", [128, 2, 8, 2], F32)
    zT = alloc("zT", [128, NCA], BF16)
    wgk_bf = alloc("wgk_bf", [128, 1024], BF16)
    G2_off = (off[0] + 31) // 32 * 32
    G2N = 6144
    G2 = alloc("G2", [128, G2N], F32)
    NSS = 3
    Ssamp = [alloc("Ssamp%d" % i, [128, 2, 256], F32) for i in range(NSS)]
    Qm = alloc("Qm", [128, 16, 48], BF16)
    Ssamp_bf = [alloc("Ssamp_bf%d" % i, [128, 2, 256], BF16) for i in range(NSS)]
    assert off[0] <= 16384 + 212000, off[0]

    def slab(i):
        return R_off + i * SLAB

    g_off = [slab(8)]
    g_end = slab(22)

    def galloc(name, shape, dt):
        nbytes = int(np.prod(shape[1:])) * (4 if dt == F32 else 2)
        o = (g_off[0] + 31) // 32 * 32
        assert o + nbytes <= g_end, (name, o + nbytes - g_end)
        g_off[0] = o + nbytes
        return alloc_at(name, shape, dt, o)

    l_h = galloc("l_h", [128, 9, 128], F32)
    EbT = galloc("EbT", [128, NCA], F32)
    ENbT = galloc("ENbT", [128, NCA], F32)
    Ec = galloc("Ec", [128, 9, 128], F32)
    tmpG = [alloc_at("tmpG%d" % i, [128, 512], F32, Ec.manual_sbuf_range[0] + i * 2048) for i in range(2)]

    class GSet:
        pass
    gsets = [GSet(), GSet()]
    gsets[0].qtil = galloc("qtil0", [128, NCA], BF16)
    gsets[0].ktil = galloc("ktil0", [128, NCA], BF16)
    gsets[0].khat = galloc("khat0", [128, 9, 128], BF16)
    gsets[0].v = galloc("v0", [128, 9, 256], BF16)
    g2 = [G2_off]
    g2_end = G2_off + G2N * 4

    def g2alloc(name, shape, dt):
        nbytes = int(np.prod(shape[1:])) * (4 if dt == F32 else 2)
        o = (g2[0] + 31) // 32 * 32
        assert o + nbytes <= g2_end, (name, o + nbytes - g2_end)
        g2[0] = o + nbytes
        return alloc_at(name, shape, dt, o)

    gsets[0].gsog = g2alloc("gsog0", [128, 9, 256], BF16)
    gsets[1].qtil = g2alloc("qtil1", [128, NCA], BF16)
    gsets[1].ktil = g2alloc("ktil1", [128, NCA], BF16)
    gsets[1].khat = g2alloc("khat1", [128, 9, 128], BF16)
    gsets[1].v = g2alloc("v1", [128, 9, 256], BF16)
    gsets[1].gsog = g2alloc("gsog1", [128, 9, 256], BF16)
    for i_ in range(2):
        gsets[i_].dec = g2alloc("dec%d" % i_, [128, 32], F32)
        gsets[i_].q32s = g2alloc("q32s%d" % i_, [128, 16], F32)
    ATs = [g2alloc("ATs%d" % i, [128, 128], BF16) for i in range(2)]
    pb_tm = [g2alloc("pb_tm%d" % i, [128, 256], BF16) for i in range(2)]
    tmp_g = [g2alloc("tmp_g%d" % i, [128, 256], F32) for i in range(2)]
    c_off = [slab(16)]

    def calloc(name, shape, dt):
        nbytes = int(np.prod(shape[1:])) * (4 if dt == F32 else 2)
        o = (c_off[0] + 31) // 32 * 32
        assert o + nbytes <= g_end, (name, o + nbytes - g_end)
        c_off[0] = o + nbytes
        return alloc_at(name, shape, dt, o)

    U = calloc("U", [128, 2 + NCA], F32)
    Ybuf = [calloc("Y%d" % i, [128, 512], F32) for i in range(2)]
    gcb = [calloc("gcb%d" % i, [128, 512], F32) for i in range(2)]
    g2c = [G2_off]

    def g2calloc(name, shape, dt):
        nbytes = int(np.prod(shape[1:])) * (4 if dt == F32 else 2)
        o = (g2c[0] + 31) // 32 * 32
        assert o + nbytes <= g2_end, (name, o + nbytes - g2_end)
        g2c[0] = o + nbytes
        return alloc_at(name, shape, dt, o)

    sc_tm = g2calloc("sc_tm", [128, 2, D], F32)
    prevT = g2calloc("prevT", [128, 16, 16], F32)
    Ys = g2calloc("Ys", [128, 16], F32)
    sgb = [alloc_at("sgb%d" % i, [128, 512], F32, slab(16) + i * 2048) for i in range(2)]
    sgf = [alloc_at("sgf%d" % i, [128, 512], F32, G2_off + i * 2048) for i in range(2)]
    us_tm = sc_tm
    pc_tm = alloc_at("pc_tm", [128, D], F32, G2_off + 4096)

    def mk(name, n):
        return [Res("%s%d" % (name, i)) for i in range(n)]

    x_res = mk("x", 9)
    xn_res = mk("xn", 9)
    slab_res = mk("slab", NJ)
    gbc_res = mk("gbc", 2)
    xnbf_res = mk("xnbf", 3)
    sqj_res = Res("sqj")
    sqj_res.strict = True
    S_res = [[Res("S%d_%d" % (l, h)) for h in range(4)] for l in range(2)]
    Sbf_res = mk("Sbf", 2)
    cst_res = Res("cst")
    idbf_res = Res("idbf")
    stat_res = mk("stat", 8)
    carry_res = [[Res("carry%d_%d" % (l, j)) for j in range(8)] for l in range(2)]
    zT_res = Res("zT")
    Ssamp_res = mk("Ssamp", NSS)
    Qm_res = Res("Qm")
    Ssbf_res = mk("Ssbf", NSS)
    bank_res = mk("bank", 8)
    g2_res = mk("g2r", G2N * 4 // 1024)

    def cells_of(t, lo_elem=None, hi_elem=None):
        lo, hi = t.manual_sbuf_range
        out = []
        if lo >= R_off and lo < R_off + NJ * SLAB:
            for i in range(NJ):
                a, b = slab(i), slab(i + 1)
                if a < hi and lo < b:
                    out.append(slab_res[i])
        elif lo >= G2_off and lo < g2_end:
            for i in range(G2N * 4 // 1024):
                a, b = G2_off + i * 1024, G2_off + (i + 1) * 1024
                if a < hi and lo < b:
                    out.append(g2_res[i])
        else:
            raise ValueError
        return out

    banks = [nc.alloc_psum_tensor("ps%d" % i, [128, 512], F32) for i in range(8)]
    banks_bf = [b.bitcast(BF16) for b in banks]
    pst = {"i": 0}

    def psum():
        i = pst["i"]
        pst["i"] = (i + 1) % 7
        return i

    def pe_warm(n=1):
        for _ in range(n):
            tns.matmul(banks[6][:, 0:128], lhsT=ident_bf[:, :], rhs=ident_bf[:, :], start=True, stop=True)
            S.pe_time += 0.06

    PE, ACT, DVE, POOL, SP = "pe", "act", "dve", "pool", "sp"
    tns, vec, sca, gps = nc.tensor, nc.vector, nc.scalar, nc.gpsimd

    def ccol(o, n=1, rows=128):
        return cst[0:rows, o:o + n]

    S.dma(SP, cst_res, [(cst[:, :], cst_d[:, :]), (cwT[:, :], conv_wt[:, :]), (wgk[0:17, :], wgk_aug[:, :]), (normw[:, :], normw_d[:, :]),
                        (gain_bc[:, :], gla_gain[0:1, :].broadcast_to([128, 512]))],
          writes=[cst_res])
    S.op(DVE, lambda: vec.tensor_copy(out=ident_bf[:, :], in_=cst[:, O_ID:O_ID + 128]), reads=[cst_res], writes=[idbf_res])
    S.op(DVE, lambda: vec.memset(stat[:, :], 1.0), writes=stat_res)
    S.op(DVE, lambda: vec.memset(zT[0:32, :], 1.0), writes=[zT_res])
    wgkbf_res = Res("wgkbf")
    KM_res = Res("KM")
    S.op(DVE, lambda: vec.tensor_copy(out=wgk_bf[0:17, :], in_=wgk[0:17, :]), reads=[cst_res, KM_res], writes=[wgkbf_res])
    KM = nc.alloc_sbuf_tensor_at("KM", [128, 16, 128], BF16, offset=wgk.manual_sbuf_range[0])

    wctr = {"n": 0, "pos": 0, "owner": [None] * NCELL, "sems": [None] * 16}

    class WT:
        pass

    def wload(pieces, kc):
        n = wctr["n"]
        wctr["n"] += 1
        ntot = sum(p.shape[1] for p in pieces)
        nel = kc * ntot
        ncell = (nel + CELL - 1) // CELL
        if wctr["pos"] + ncell > NCELL:
            wctr["pos"] = 0
        pos = wctr["pos"]
        wctr["pos"] = pos + ncell
        res = Res("w%d" % n)
        acc = {}
        for c in range(pos, pos + ncell):
            o = wctr["owner"][c]
            if o is not None:
                evs = dict(o.r)
                if o.w is not None:
                    evs[o.w[0]] = max(evs.get(o.w[0], 0), o.w[1])
                for k_, v_ in evs.items():
                    if acc.get(k_, 0) < v_:
                        acc[k_] = v_
            wctr["owner"][c] = res
        res.r = acc
        si = n % 16
        if wctr["sems"][si] is None:
            wctr["sems"][si] = [es.enter_context(nc.semaphore("d_w%d" % si)), 0]
        res.dsem, res.dcnt, res.dkey = wctr["sems"][si][0], wctr["sems"][si][1], ("d", "wsem%d" % si)
        dv = wring[:, pos * CELL:pos * CELL + nel].rearrange("p (k c) -> p k c", k=kc)
        pairs = []
        c = 0
        for p in pieces:
            w = p.shape[1]
            pairs.append((dv[:, :, c:c + w], p.rearrange("(k p) c -> p k c", p=128)))
            c += w
        S.dma(POOL, res, pairs, writes=[res])
        wctr["sems"][si][1] = res.dcnt
        w = WT()
        w.ap = dv
        w.res = res
        return w

    def mm_group(bank, rows, n, lhs_list, rhs_list, reads, c0=0):
        def fn():
            ins = None
            m = len(lhs_list)
            for i in range(m):
                ins = tns.matmul(banks[bank][0:rows, c0:c0 + n], lhsT=lhs_list[i], rhs=rhs_list[i],
                                 start=(i == 0), stop=(i == m - 1))
            return ins
        S.op(PE, fn, reads=reads, writes=[bank_res[bank]], cost=len(lhs_list) * max(n, 64) / 2400.0)

    def load_x(H):
        b = None
        if H.name == "A":
            S.op(DVE, lambda: vec.memset(x_tm[0:48, 0, :], 0.0), writes=[x_res[0]])
            S.dma(SP, x_res[0], [(x_tm[0:16, 0, :], x_sample[:, :]), (x_tm[32:48, 0, :], meta[:, :])], writes=[x_res[0]])
        for r in range(1, 9):
            t0 = H.tok0 + 128 * (r - 1)
            S.dma(SP, x_res[r], [(x_tm[:, r, :], x_prompt[t0:t0 + 128, :])], writes=[x_res[r]])

    gst = {"i": 0, "xb": 0}

    nst_res = mk("nst", 9)

    def load_g(gvec):
        gi = gst["i"]
        gst["i"] ^= 1
        S.dma(SP, gbc_res[gi], [(gbc[gi][:, :], gvec.broadcast_to([128, D]))], writes=[gbc_res[gi]])
        return gi

    def norm_block_a(H, bi, c0, rows, gi, final):
        ss, var, lnv, rstd = (stat[0:rows, o + bi:o + bi + 1] for o in (0, 9, 18, 27))
        S.op(ACT, lambda: sca.activation(out=sqj[0:rows, :], in_=x_tm[0:rows, bi, :], func=AF.Square, accum_out=ss),
             reads=[x_res[bi]], writes=[sqj_res, nst_res[bi]])
        S.op(ACT, lambda: sca.activation(out=lnv, in_=ss, func=AF.Ln, scale=1.0 / D, bias=ccol(O_EPS, 1, rows)),
             reads=[nst_res[bi], cst_res], writes=[nst_res[bi]])
        S.op(ACT, lambda: sca.activation(out=rstd, in_=lnv, func=AF.Exp, scale=-0.5), reads=[nst_res[bi]], writes=[nst_res[bi]])
        if final:
            S.op(DVE, lambda: vec.scalar_tensor_tensor(out=x_tm[0:rows, bi, :], in0=x_tm[0:rows, bi, :], scalar=rstd, in1=gbc[gi][0:rows, :],
                                                       op0=ALU.mult, op1=ALU.mult),
                 reads=[x_res[bi], nst_res[bi], gbc_res[gi]], writes=[x_res[bi]])
            if bi == 0:
                S.dma(ACT, x_res[0], [(y_sample[:, :], x_tm[0:16, 0, :])], reads=[x_res[0]], is_output=True)
            else:
                t0 = H.tok0 + 128 * (bi - 1)
                S.dma(ACT, x_res[bi], [(y_prompt[t0:t0 + 128, :], x_tm[:, bi, :])], reads=[x_res[bi]], is_output=True)
            return None
        xb = gst["xb"]
        gst["xb"] = (xb + 1) % 3
        S.op(ACT, lambda: sca.activation(out=xn_bf[xb][0:rows, :], in_=x_tm[0:rows, bi, :], func=AF.Copy, scale=rstd),
             reads=[x_res[bi], nst_res[bi]], writes=[xnbf_res[xb]])
        return xb

    def norm_block_b(bi, c0, rows, xb, wj):
        bk = psum()

        def fn():
            ins = None
            for k in range(KC):
                ins = tns.transpose(out=banks_bf[bk][:, k * 128:k * 128 + rows], in_=xn_bf[xb][0:rows, k * 128:(k + 1) * 128],
                                    identity=ident_bf[0:rows, 0:rows])
            return ins
        S.op(PE, fn, reads=[xnbf_res[xb], idbf_res], writes=[bank_res[bk]], cost=0.06 * KC)
        src = banks_bf[bk][:, :].rearrange("p (k c) -> p k c", k=KC)[:, :, 0:rows]
        gT = normw[:, wj * 8:(wj + 1) * 8].unsqueeze(2).broadcast_to([128, KC, rows])
        S.op(DVE, lambda: vec.tensor_tensor(out=xnT[:, :, c0:c0 + rows], in0=src, in1=gT, op=ALU.mult),
             reads=[bank_res[bk], cst_res], writes=[xn_res[bi]])

    class NormPipe:
        def __init__(self, H, gvec, final=False, depth=2, wj=None):
            self.H, self.final, self.depth, self.wj = H, final, depth, wj
            self.gi = load_g(gvec) if final else None
            self.pend = []

        def block(self, bi, c0, rows):
            xb = norm_block_a(self.H, bi, c0, rows, self.gi, self.final)
            if not self.final:
                self.pend.append((bi, c0, rows, xb, self.wj))
            while len(self.pend) > self.depth:
                norm_block_b(*self.pend.pop(0))

        def flush(self):
            while self.pend:
                norm_block_b(*self.pend.pop(0))

    def norm_phase(H, gvec, wj):
        npipe = NormPipe(H, gvec, wj=wj)
        for (bi, c0, rows) in H.blocks:
            npipe.block(bi, c0, rows)
        npipe.flush()

    def xn_reads(H, c0, n):
        return [xn_res[b] for b in H.blk_of(c0, n)]

    def alias_begin(newres, oldres):
        acc = {}
        for o in oldres:
            if o.w is not None:
                k, v = o.w
                if acc.get(k, 0) < v:
                    acc[k] = v
            for k, v in o.r.items():
                if acc.get(k, 0) < v:
                    acc[k] = v
        for n_ in newres:
            n_.w = None
            n_.r = dict(acc)

    def alias_end(newres, oldres):
        acc = {}
        for o in newres:
            if o.w is not None:
                k, v = o.w
                if acc.get(k, 0) < v:
                    acc[k] = v
            for k, v in o.r.items():
                if acc.get(k, 0) < v:
                    acc[k] = v
        for o in oldres:
            for k, v in acc.items():
                if o.r.get(k, 0) < v:
                    o.r[k] = v

    def gla_phase(H, l):
        isA = H.name == "A"
        newres = []

        def mkr(name, n):
            rs = mk(name, n)
            newres.extend(rs)
            return rs
        lh_res, eb_res, ec_res = mkr("lh", 9), mkr("eb", 9), mkr("ec", 9)
        at_res, pb_res, tg_res = mkr("at", 2), mkr("pb", 2), mkr("tg", 2)
        tgG_res = mkr("tgG", 2)
        for i_, st in enumerate(gsets):
            st.q_res, st.kh_res, st.v_res, st.g_res = mkr("q%d_" % i_, 9), mkr("kh%d_" % i_, 9), mkr("v%d_" % i_, 9), mkr("g%d_" % i_, 9)
            st.dec_res = mkr("dec%d_" % i_, 1)[0]
            st.q32_res = mkr("q32%d_" % i_, 1)[0]
        oldres = slab_res[8:22] + g2_res
        alias_begin(newres, oldres)

        wz = wload([w_in[l, :, C_Z:C_Z + 16]], KC)
        for (c0, n) in H.tiles:
            bk = psum()
            mm_group(bk, 16, n, [wz.ap[:, k, 0:16] for k in range(KC)], [xnT[:, k, c0:c0 + n] for k in range(KC)],
                     reads=[wz.res] + xn_reads(H, c0, n))
            S.op(ACT, lambda: sca.copy(out=zT[0:16, c0:c0 + n], in_=banks[bk][0:16, 0:n]), reads=[bank_res[bk]], writes=[zT_res])

        def P_steps(h, st):
            steps = []
            hold = {}

            def f_load():
                hold["qk"] = wload([w_in[l, :, C_Q + h * 128:C_Q + (h + 1) * 128], w_in[l, :, C_K + h * 128:C_K + (h + 1) * 128]], KC)
                hold["v"] = wload([w_in[l, :, C_V + h * 256:C_V + (h + 1) * 256]], KC)
                hold["og"] = wload([w_in[l, :, C_OG + h * 256:C_OG + (h + 1) * 256]], KC)
            steps.append(f_load)
            s_gate, s_dec, s_v, s_og = [], [], [], []
            groups = []
            if H.blocks[0][0] == 0:
                groups.append([H.blocks[0]])
            realb = [b_ for b_ in H.blocks if b_[0] != 0]
            for i_ in range(0, len(realb), 4):
                groups.append(realb[i_:i_ + 4])
            for gi_, grp in enumerate(groups):
                def f(grp=grp, gi_=gi_):
                    rows = grp[0][2]
                    bi0 = grp[0][0]
                    ng = len(grp)
                    n = ng * 128
                    bk = psum()

                    def fn():
                        ins = None
                        for j, (bi, c0, r_) in enumerate(grp):
                            ins = tns.matmul(banks[bk][0:rows, j * 128:(j + 1) * 128], lhsT=zT[0:17, c0:c0 + rows],
                                             rhs=wgk_bf[0:17, l * 512 + h * 128: l * 512 + (h + 1) * 128], start=True, stop=True)
                        return ins
                    S.op(PE, fn, reads=[zT_res, wgkbf_res], writes=[bank_res[bk]], cost=0.06 * ng)
                    tg = tmpG[gi_ % 2]
                    S.op(ACT, lambda: sca.activation(out=tg[0:rows, 0:n], in_=banks[bk][0:rows, 0:n], func=AF.Exp, scale=-1.0),
                         reads=[bank_res[bk]], writes=[tgG_res[gi_ % 2]])
                    S.op(ACT, lambda: sca.activation(out=l_h[0:rows, bi0:bi0 + ng, :].rearrange("p a b -> p (a b)"), in_=tg[0:rows, 0:n], func=AF.Ln,
                                                     bias=ccol(O_ONE, 1, rows), scale=1.0),
                         reads=[tgG_res[gi_ % 2], cst_res], writes=[lh_res[b_[0]] for b_ in grp])
                s_gate.append(f)
            for gi_, grp in enumerate(groups):
                def f(grp=grp, gi_=gi_):
                    rows = grp[0][2]
                    ng = len(grp)
                    cfirst = grp[0][1]
                    n = (ng - 1) * 128 + rows
                    o1 = O_TRIE if grp[0][0] == 0 else O_TRIR
                    tri = cst[0:rows, o1:o1 + rows]
                    bk = psum()

                    def fn():
                        ins = None
                        for j, (bi, c0, r_) in enumerate(grp):
                            ins = tns.matmul(banks[bk][:, j * 128:j * 128 + rows], lhsT=l_h[0:rows, bi, :], rhs=tri, start=True, stop=True)
                        return ins
                    S.op(PE, fn, reads=[lh_res[b_[0]] for b_ in grp] + [cst_res], writes=[bank_res[bk]], cost=0.22 * ng)
                    ebr = [eb_res[b_[0]] for b_ in grp]
                    S.op(ACT, lambda: sca.activation(out=EbT[:, cfirst:cfirst + n], in_=banks[bk][:, 0:n], func=AF.Exp),
                         reads=[bank_res[bk]], writes=ebr)
                    S.op(ACT, lambda: sca.activation(out=ENbT[:, cfirst:cfirst + n], in_=banks[bk][:, 0:n], func=AF.Exp, scale=-1.0),
                         reads=[bank_res[bk]], writes=ebr)
                    for j, (bi, c0, r_) in enumerate(grp):
                        S.op(ACT, lambda: sca.copy(out=st.dec[:, 16 + bi:17 + bi], in_=EbT[:, c0 + rows - 1:c0 + rows]),
                             reads=ebr, writes=[st.dec_res])
                    if grp[0][0] == 0:
                        S.op(ACT, lambda: sca.copy(out=st.dec[:, 0:16], in_=EbT[:, 0:16]), reads=ebr, writes=[st.dec_res])
                s_dec.append(f)
            groups2 = []
            if H.blocks[0][0] == 0:
                groups2.append([H.blocks[0]])
            for i_ in range(0, len(realb), 2):
                groups2.append(realb[i_:i_ + 2])
            for gi_, grp in enumerate(groups2):
                def f(grp=grp, gi_=gi_):
                    wv = hold["v"]
                    rows = grp[0][2]
                    bi0 = grp[0][0]
                    ng = len(grp)
                    bk = psum()

                    def fn():
                        ins = None
                        for j, (bi, c0, r_) in enumerate(grp):
                            for k in range(KC):
                                ins = tns.matmul(banks[bk][0:rows, j * 256:(j + 1) * 256], lhsT=xnT[:, k, c0:c0 + rows], rhs=wv.ap[:, k, :],
                                                 start=(k == 0), stop=(k == KC - 1))
                        return ins
                    S.op(PE, fn, reads=[wv.res] + [xn_res[b_[0]] for b_ in grp], writes=[bank_res[bk]], cost=ng * KC * 256 / 2400.0)
                    S.op(ACT, lambda: sca.copy(out=st.v[0:rows, bi0:bi0 + ng, :].rearrange("p a b -> p (a b)"), in_=banks[bk][0:rows, 0:ng * 256]),
                         reads=[bank_res[bk]], writes=[st.v_res[b_[0]] for b_ in grp])
                s_v.append(f)
            for gi_, grp in enumerate(groups2):
                def f(grp=grp, gi_=gi_):
                    wog = hold["og"]
                    rows = grp[0][2]
                    bi0 = grp[0][0]
                    ng = len(grp)
                    n = ng * 256
                    bk = psum()

                    def fn():
                        ins = None
                        for j, (bi, c0, r_) in enumerate(grp):
                            for k in range(KC):
                                ins = tns.matmul(banks[bk][0:rows, j * 256:(j + 1) * 256], lhsT=xnT[:, k, c0:c0 + rows], rhs=wog.ap[:, k, :],
                                                 start=(k == 0), stop=(k == KC - 1))
                        return ins
                    S.op(PE, fn, reads=[wog.res] + [xn_res[b_[0]] for b_ in grp], writes=[bank_res[bk]], cost=ng * KC * 256 / 2400.0)
                    tg = tmpG[gi_ % 2]
                    tr = [tgG_res[gi_ % 2]]
                    S.op(ACT, lambda: sca.activation(out=tg[0:rows, 0:n], in_=banks[bk][0:rows, 0:n], func=AF.Exp, scale=-1.0),
                         reads=[bank_res[bk]], writes=tr)
                    S.op(ACT, lambda: sca.activation(out=tg[0:rows, 0:n], in_=tg[0:rows, 0:n], func=AF.Ln, bias=ccol(O_ONE, 1, rows), scale=1.0),
                         reads=tr + [cst_res], writes=tr)
                    S.op(ACT, lambda: sca.activation(out=tg[0:rows, 0:n], in_=tg[0:rows, 0:n], func=AF.Exp, scale=-1.0), reads=tr, writes=tr)
                    S.op(DVE, lambda: vec.tensor_tensor(out=tg[0:rows, 0:n], in0=banks[bk][0:rows, 0:n], in1=tg[0:rows, 0:n], op=ALU.mult),
                         reads=[bank_res[bk]] + tr, writes=tr)
                    gb_ = gain_bc[0:rows, l * 256:(l + 1) * 256].unsqueeze(1).broadcast_to([rows, ng, 256])
                    S.op(DVE, lambda: vec.tensor_tensor(out=st.gsog[0:rows, bi0:bi0 + ng, :], in0=tg[0:rows, 0:n].rearrange("p (a b) -> p a b", a=ng),
                                                        in1=gb_, op=ALU.mult),
                         reads=tr + [cst_res], writes=[st.g_res[b_[0]] for b_ in grp])
                s_og.append(f)
            s_kh = []
            s_qk = []
            for (c0, n) in H.tiles:
                def f(c0=c0, n=n):
                    wqk = hold["qk"]
                    blks = H.blk_of(c0, n)
                    bq = psum()
                    mm_group(bq, 128, n, [wqk.ap[:, k, 0:128] for k in range(KC)], [xnT[:, k, c0:c0 + n] for k in range(KC)],
                             reads=[wqk.res] + xn_reads(H, c0, n))
                    S.op(DVE, lambda: vec.scalar_tensor_tensor(out=st.qtil[:, c0:c0 + n], in0=banks[bq][:, 0:n], scalar=128.0 ** -0.5,
                                                               in1=EbT[:, c0:c0 + n], op0=ALU.mult, op1=ALU.mult),
                         reads=[bank_res[bq]] + [eb_res[b] for b in blks], writes=[st.q_res[b] for b in blks])
                    if isA and c0 == 0:
                        S.op(DVE, lambda: vec.tensor_scalar(out=st.q32s[:, 0:16], in0=banks[bq][:, 0:16], scalar1=128.0 ** -0.5, scalar2=None,
                                                            op0=ALU.mult),
                             reads=[bank_res[bq]], writes=[st.q32_res])
                def f2(c0=c0, n=n):
                    wqk = hold["qk"]
                    blks = H.blk_of(c0, n)
                    bkk = psum()
                    mm_group(bkk, 128, n, [wqk.ap[:, k, 128:256] for k in range(KC)], [xnT[:, k, c0:c0 + n] for k in range(KC)],
                             reads=[wqk.res] + xn_reads(H, c0, n))
                    S.op(DVE, lambda: vec.tensor_tensor(out=st.ktil[:, c0:c0 + n], in0=banks[bkk][:, 0:n], in1=ENbT[:, c0:c0 + n], op=ALU.mult),
                         reads=[bank_res[bkk]] + [eb_res[b] for b in blks], writes=[st.q_res[b] for b in blks])
                s_qk.append(f)
                s_qk.append(f2)
            for gi_, grp in enumerate(groups):
                def f(grp=grp, gi_=gi_):
                    rows = grp[0][2]
                    bi0 = grp[0][0]
                    ng = len(grp)
                    bk = psum()

                    def fn():
                        ins = None
                        for j, (bi, c0, r_) in enumerate(grp):
                            ins = tns.transpose(out=banks_bf[bk][0:rows, j * 128:(j + 1) * 128], in_=st.ktil[:, c0:c0 + rows], identity=ident_bf[:, :])
                        return ins
                    S.op(PE, fn, reads=[st.q_res[b_[0]] for b_ in grp] + [idbf_res], writes=[bank_res[bk]], cost=0.06 * ng)
                    S.op(ACT, lambda: sca.copy(out=st.khat[0:rows, bi0:bi0 + ng, :].rearrange("p a b -> p (a b)"), in_=banks_bf[bk][0:rows, 0:ng * 128]),
                         reads=[bank_res[bk]], writes=[st.kh_res[b_[0]] for b_ in grp])
                s_kh.append(f)
            steps.extend(s_gate)
            steps.extend(s_v)
            steps.extend(s_dec)
            steps.extend(s_qk)
            steps.extend(s_kh)
            steps.extend(s_og)
            return steps

        def R_steps(h, st):
            sres = S_res[l][h]
            Sap = Sst[:, l, h, :]
            state = {"sbf": None, "bo": {}, "i": 0, "pdec": None}
            blocks = H.blocks
            nb = len(blocks)

            def next_sbf():
                state["i"] ^= 1
                return state["i"]

            def s1(bi, c0, rows):
                ats = ATs[bi % 2]
                ba = psum()
                S.op(PE, lambda: tns.matmul(banks[ba][0:rows, 0:rows], lhsT=st.ktil[:, c0:c0 + rows], rhs=st.qtil[:, c0:c0 + rows], start=True, stop=True),
                     reads=[st.q_res[bi]], writes=[bank_res[ba]])
                mo = O_MCE if bi == 0 else O_MCR
                S.op(DVE, lambda: vec.tensor_tensor(out=ats[0:rows, 0:rows], in0=banks[ba][0:rows, 0:rows], in1=cst[0:rows, mo:mo + rows], op=ALU.mult),
                     reads=[bank_res[ba], cst_res], writes=[at_res[bi % 2]])
                if bi == 0:
                    S.op(DVE, lambda: vec.tensor_tensor(out=Qm[:, :, :], in0=st.q32s[:, 0:16].unsqueeze(2).broadcast_to([128, 16, 48]),
                                                        in1=cst[:, O_DELTA:O_DELTA + 768].rearrange("p (s j) -> p s j", s=16), op=ALU.mult),
                         reads=[st.q32_res, cst_res], writes=[Qm_res])
                    S.op(DVE, lambda: vec.tensor_tensor(out=KM[0:16, :, :], in0=st.khat[0:16, 0:1, :].broadcast_to([16, 16, 128]),
                                                        in1=cst[0:16, O_ID:O_ID + 16].unsqueeze(2).broadcast_to([16, 16, 128]), op=ALU.mult),
                         reads=[st.kh_res[0], cst_res], writes=[KM_res])

            def s2_real(bi, c0, rows, last):
                ats = ATs[bi % 2]
                if state["sbf"] is None:
                    i = next_sbf()
                    S.op(ACT, lambda: sca.copy(out=S_bf[i][:, :], in_=Sap), reads=[sres], writes=[Sbf_res[i]])
                    state["sbf"] = i
                cur = state["sbf"]
                bo = psum()
                state["bo"][bi] = bo

                def fn():
                    tns.matmul(banks[bo][0:rows, 0:256], lhsT=ats[0:rows, 0:rows], rhs=st.v[0:rows, bi, :], start=True, stop=False)
                    return tns.matmul(banks[bo][0:rows, 0:256], lhsT=st.qtil[:, c0:c0 + rows], rhs=S_bf[cur][:, :], start=False, stop=True)
                S.op(PE, fn, reads=[at_res[bi % 2], st.v_res[bi], st.q_res[bi], Sbf_res[cur]], writes=[bank_res[bo]])
                bu = psum()
                S.op(PE, lambda: tns.matmul(banks[bu][:, 0:256], lhsT=st.khat[0:rows, bi, :], rhs=st.v[0:rows, bi, :], start=True, stop=True),
                     reads=[st.kh_res[bi], st.v_res[bi]], writes=[bank_res[bu]])
                dcol = st.dec[:, 16 + bi:17 + bi]
                pd = state["pdec"]
                if pd is None:
                    S.op(DVE, lambda: vec.tensor_tensor(out=Sap, in0=banks[bu][:, 0:256], in1=Sap, op=ALU.add),
                         reads=[sres, bank_res[bu]], writes=[sres])
                else:
                    S.op(DVE, lambda: vec.scalar_tensor_tensor(out=Sap, in0=Sap, scalar=pd, in1=banks[bu][:, 0:256], op0=ALU.mult, op1=ALU.add),
                         reads=[sres, bank_res[bu], st.dec_res], writes=[sres])
                if not last:
                    i = next_sbf()
                    S.op(DVE, lambda: vec.tensor_scalar(out=S_bf[i][:, :], in0=Sap, scalar1=dcol, scalar2=None, op0=ALU.mult),
                         reads=[sres, st.dec_res], writes=[Sbf_res[i]])
                    state["sbf"] = i
                    state["pdec"] = dcol
                else:
                    S.op(DVE, lambda: vec.tensor_scalar(out=Sap, in0=Sap, scalar1=dcol, scalar2=None, op0=ALU.mult),
                         reads=[sres, st.dec_res], writes=[sres])
                    state["pdec"] = None

            def s2_E_begin(bi, c0, rows):
                ats = ATs[bi % 2]
                bo = 7
                state["bo"][bi] = bo
                S.op(PE, lambda: tns.matmul(banks[bo][0:48, 0:256], lhsT=ats[32:48, 0:48], rhs=st.v[32:48, 0, :], start=True, stop=False),
                     reads=[at_res[bi % 2], st.v_res[0]], writes=[bank_res[bo]])

            def s2_E_pair_a(p):
                sb = p % NSS
                S.dma(SP, Ssamp_res[sb], [(Ssamp[sb][:, :, :], state_gla[l, 2 * p:2 * p + 2, h, :, :].rearrange("s d v -> d s v"))],
                      writes=[Ssamp_res[sb]])
                bu = psum()

                def fn_u():
                    ins = None
                    for i in range(2):
                        ins = tns.matmul(banks[bu][:, i * 256:(i + 1) * 256], lhsT=KM[0:16, 2 * p + i, :], rhs=st.v[0:16, 0, :], start=True, stop=True)
                    return ins
                S.op(PE, fn_u, reads=[KM_res, st.v_res[0]], writes=[bank_res[bu]])
                S.op(DVE, lambda: vec.tensor_tensor(out=Ssamp[sb][:, :, :].rearrange("p s v -> p (s v)"), in0=banks[bu][:, 0:512],
                                                    in1=Ssamp[sb][:, :, :].rearrange("p s v -> p (s v)"), op=ALU.add),
                     reads=[Ssamp_res[sb], bank_res[bu]], writes=[Ssamp_res[sb]])
                for i in range(2):
                    s_ = 2 * p + i
                    S.op(DVE, lambda: vec.tensor_scalar(out=Ssamp_bf[sb][:, i, :], in0=Ssamp[sb][:, i, :], scalar1=st.dec[:, s_:s_ + 1], scalar2=None, op0=ALU.mult),
                         reads=[Ssamp_res[sb], st.dec_res], writes=[Ssbf_res[sb]])
                    S.op(DVE, lambda: vec.tensor_scalar(out=Ssamp[sb][:, i, :], in0=Ssamp[sb][:, i, :], scalar1=st.dec[:, s_:s_ + 1], scalar2=None, op0=ALU.mult),
                         reads=[Ssamp_res[sb], st.dec_res], writes=[Ssamp_res[sb]])
                S.dma(ACT, Ssamp_res[sb], [(s_gla[l, 2 * p:2 * p + 2, h, :, :].rearrange("s d v -> d s v"), Ssamp[sb][:, :, :])],
                      reads=[Ssamp_res[sb]], is_output=True)

            def s2_E_pair_b(p):
                bo = 7
                sb = p % NSS

                def fn():
                    ins = None
                    for i in range(2):
                        s_ = 2 * p + i
                        ins = tns.matmul(banks[bo][0:48, 0:256], lhsT=Qm[:, s_, :], rhs=Ssamp_bf[sb][:, i, :], start=False, stop=(s_ == 15))
                    return ins
                S.op(PE, fn, reads=[Qm_res, Ssbf_res[sb]], writes=[bank_res[bo]])

            def s2_E_end(bi, c0, rows):
                bu = psum()
                S.op(PE, lambda: tns.matmul(banks[bu][:, 0:256], lhsT=st.khat[32:48, 0, :], rhs=st.v[32:48, 0, :], start=True, stop=True),
                     reads=[st.kh_res[0], st.v_res[0]], writes=[bank_res[bu]])
                i = next_sbf()
                dE = st.dec[:, 16:17]
                S.op(DVE, lambda: vec.tensor_scalar(out=S_bf[i][:, :], in0=banks[bu][:, 0:256], scalar1=dE, scalar2=None, op0=ALU.mult),
                     reads=[bank_res[bu], st.dec_res], writes=[Sbf_res[i]])
                state["sbf"] = i
                S.op(DVE, lambda: vec.tensor_scalar(out=Sap, in0=banks[bu][:, 0:256], scalar1=dE, scalar2=None, op0=ALU.mult),
                     reads=[bank_res[bu], st.dec_res], writes=[sres])

            def s3(bi, c0, rows):
                bo = state["bo"][bi]
                S.op(ACT, lambda: sca.activation(out=sqj[0:rows, 0:256], in_=banks[bo][0:rows, 0:256], func=AF.Square,
                                                 accum_out=stat[0:rows, 40:41]),
                     reads=[bank_res[bo]], writes=[sqj_res, stat_res[4]])
                S.op(ACT, lambda: sca.activation(out=stat[0:rows, 42:43], in_=stat[0:rows, 40:41], func=AF.Ln, scale=1.0 / 256,
                                                 bias=ccol(O_EPS, 1, rows)),
                     reads=[stat_res[4], cst_res], writes=[stat_res[6]])
                S.op(ACT, lambda: sca.activation(out=stat[0:rows, 43:44], in_=stat[0:rows, 42:43], func=AF.Exp, scale=-0.5),
                     reads=[stat_res[6]], writes=[stat_res[7]])
                pb = pb_tm[bi % 2]
                S.op(DVE, lambda: vec.scalar_tensor_tensor(out=pb[0:rows, :], in0=banks[bo][0:rows, 0:256], scalar=stat[0:rows, 43:44],
                                                           in1=st.gsog[0:rows, bi, :], op0=ALU.mult, op1=ALU.mult),
                     reads=[bank_res[bo], stat_res[7], st.g_res[bi]], writes=[pb_res[bi % 2]])

            def s4(bi, c0, rows):
                pb = pb_tm[bi % 2]
                bt = psum()

                def fn():
                    ins = None
                    for i in range(2):
                        ins = tns.transpose(out=banks_bf[bt][:, i * 128:i * 128 + rows], in_=pb[0:rows, i * 128:(i + 1) * 128],
                                            identity=ident_bf[0:rows, 0:rows])
                    return ins
                S.op(PE, fn, reads=[pb_res[bi % 2], idbf_res], writes=[bank_res[bt]])
                src = banks_bf[bt][:, 0:256].rearrange("p (k c) -> p k c", k=2)[:, :, 0:rows]
                S.op(DVE, lambda: vec.tensor_copy(out=Rt[:, 2 * h:2 * h + 2, c0:c0 + rows], in_=src),
                     reads=[bank_res[bt]], writes=[slab_res[2 * h], slab_res[2 * h + 1]])

            LM, LS, LE, LP = 2.5, 3.0, 8.0, 4.0
            steps = []
            real = [b for b in blocks if b[0] != 0]
            nr = len(real)
            hasE = blocks[0][0] == 0
            steps.append(("s1_%d" % blocks[0][0], lambda: s1(*blocks[0]), []))
            if hasE:
                steps.append(("s1_%d" % real[0][0], lambda b=real[0]: s1(*b), []))
                steps.append(("Eb", lambda b=blocks[0]: s2_E_begin(*b), [("s1_0", LM)]))
                steps.append(("s2_0", lambda b=blocks[0]: s2_E_end(*b), []))
            for idx, (bi, c0, rows) in enumerate(real):
                if idx + 1 < nr:
                    steps.append(("s1_%d" % real[idx + 1][0], lambda b=real[idx + 1]: s1(*b), []))
                if isA:
                    steps.append(("a%d" % idx, lambda p=idx: s2_E_pair_a(p), [("a%d" % (idx - NSS), 6.0)] if idx >= NSS else []))
                prev = ("s2_%d" % real[idx - 1][0]) if idx >= 1 else ("s2_0" if hasE else None)
                deps = [("s1_%d" % bi, LM)] + ([(prev, LS)] if prev else [])

                def f2(b=real[idx], last=(idx == nr - 1)):
                    s2_real(b[0], b[1], b[2], last)
                    s3(*b)
                steps.append(("s2_%d" % bi, f2, deps))
                if isA and idx >= 1:
                    steps.append(("b%d" % (idx - 1), lambda p=idx: s2_E_pair_b(p - 1), [("a%d" % (idx - 1), LP)]))
                if idx >= 1:
                    steps.append(("s4_%d" % real[idx - 1][0], lambda b=real[idx - 1]: s4(*b), [("s2_%d" % real[idx - 1][0], LE)]))
            steps.append(("s4_%d" % real[nr - 1][0], lambda b=real[nr - 1]: s4(*b), [("s2_%d" % real[nr - 1][0], LE)]))
            if isA:
                def fE():
                    s2_E_pair_b(7)
                    s3(*blocks[0])
                steps.append(("b7", fE, [("a7", LP)]))
                steps.append(("s4_0", lambda b=blocks[0]: s4(*b), [("b7", LE)]))
            if H.name == "B":
                steps.append(("pgla", lambda: S.dma(ACT, sres, [(p_gla[l, h, :, :], Sap)], reads=[sres], is_output=True), []))
            return steps

        def merge(rs, ps):
            issued = {}
            pi = 0
            for (name, fn, deps) in rs:
                t_ready = max([issued.get(d, -1e9) + lat for d, lat in deps] + [-1e9])
                while S.pe_time < t_ready and pi < len(ps):
                    ps[pi]()
                    pi += 1
                issued[name] = S.pe_time
                fn()
            while pi < len(ps):
                ps[pi]()
                pi += 1

        for f in P_steps(0, gsets[0]):
            f()
        for h in range(4):
            rs = R_steps(h, gsets[h % 2])
            if h < 3:
                ps = P_steps(h + 1, gsets[(h + 1) % 2])
            else:
                dead = lh_res + eb_res + ec_res + tgG_res + gsets[0].q_res + gsets[0].kh_res + gsets[0].v_res
                alias_end(dead, slab_res[8:22])
                ps = gate_steps(H, l, C_GBB)
            merge(rs, ps)
        alias_end(newres, oldres)

    def gate_steps(H, l, gcol):
        steps = []
        hold = {}
        for jp in range(4):
            def fl(jp=jp):
                hold[jp] = wload([w_in[l, :, gcol + jp * 256:gcol + (jp + 1) * 256]], KC)
            steps.append(fl)
            for jj in range(2):
                for (c0, n) in H.tiles:
                    def f(jp=jp, jj=jj, c0=c0, n=n):
                        w = hold[jp]
                        jo = 2 * jp + jj
                        bg = psum()
                        mm_group(bg, 128, n, [w.ap[:, k, jj * 128:(jj + 1) * 128] for k in range(KC)], [xnT[:, k, c0:c0 + n] for k in range(KC)],
                                 reads=[w.res] + xn_reads(H, c0, n))
                        tb = sgb[(jo + c0 // 512) % 2]
                        tres = cells_of(tb)
                        S.op(ACT, lambda: sca.activation(out=tb[:, 0:n], in_=banks[bg][:, 0:n], func=AF.Exp, scale=-1.0),
                             reads=[bank_res[bg]], writes=tres)
                        S.op(ACT, lambda: sca.activation(out=tb[:, 0:n], in_=tb[:, 0:n], func=AF.Ln, bias=ccol(O_ONE), scale=1.0),
                             reads=tres + [cst_res], writes=tres)
                        S.op(ACT, lambda: sca.activation(out=Rt[:, 8 + jo, c0:c0 + n], in_=tb[:, 0:n], func=AF.Exp, scale=-1.0),
                             reads=tres, writes=[slab_res[8 + jo]])
                    steps.append(f)
        return steps

    def yb_phase(H, l):
        for jo in range(8):
            w = wload([w_b[l, :, jo * 128:(jo + 1) * 128]], KC)
            for (c0, n) in H.tiles:
                by = psum()
                mm_group(by, 128, n, [w.ap[:, k, 0:128] for k in range(KC)], [Rt[:, k, c0:c0 + n] for k in range(KC)],
                         reads=[w.res] + [slab_res[k] for k in range(KC)])
                S.op(DVE, lambda: vec.tensor_tensor(out=Rt[:, 8 + jo, c0:c0 + n], in0=banks[by][:, 0:n], in1=Rt[:, 8 + jo, c0:c0 + n], op=ALU.mult),
                     reads=[bank_res[by], slab_res[8 + jo]], writes=[slab_res[8 + jo]])

    def gate_out_phase(H, l, wmat, gcol, final):
        tbuf = sgb
        tres_all = [cells_of(t) for t in tbuf]
        cnt = 0
        for jo in range(8):
            w = wload([wmat[l, :, jo * 128:(jo + 1) * 128], w_in[l, :, gcol + jo * 128:gcol + (jo + 1) * 128]], KC)
            for (c0, n) in H.tiles:
                by = psum()
                mm_group(by, 128, n, [w.ap[:, k, 0:128] for k in range(KC)], [Rt[:, k, c0:c0 + n] for k in range(KC)],
                         reads=[w.res] + [slab_res[k] for k in range(KC)])
                bg = psum()
                mm_group(bg, 128, n, [w.ap[:, k, 128:256] for k in range(KC)], [xnT[:, k, c0:c0 + n] for k in range(KC)],
                         reads=[w.res] + xn_reads(H, c0, n))
                tb = tbuf[cnt % 2]
                tres = tres_all[cnt % 2]
                cnt += 1
                S.op(ACT, lambda: sca.activation(out=tb[:, 0:n], in_=banks[bg][:, 0:n], func=AF.Sigmoid), reads=[bank_res[bg]], writes=tres)
                if not final:
                    S.op(DVE, lambda: vec.tensor_tensor(out=Rt[:, 8 + jo, c0:c0 + n], in0=banks[by][:, 0:n], in1=tb[:, 0:n], op=ALU.mult),
                         reads=[bank_res[by]] + tres, writes=[slab_res[8 + jo]])
                else:
                    S.op(DVE, lambda: vec.tensor_tensor(out=tb[:, 0:n], in0=banks[by][:, 0:n], in1=tb[:, 0:n], op=ALU.mult),
                         reads=[bank_res[by]] + tres, writes=tres)
                    S.op(DVE, lambda: vec.tensor_tensor(out=Rt[:, 8 + jo, c0:c0 + n], in0=tb[:, 0:n], in1=Rt[:, 8 + jo, c0:c0 + n], op=ALU.add),
                         reads=tres + [slab_res[8 + jo]], writes=[slab_res[8 + jo]])

    def conv_phase(H, l):
        isA = H.name == "A"
        ures = cells_of(U)
        if isA:
            scres = cells_of(sc_tm)
            pres = cells_of(prevT)
            S.dma(SP, scres[0], [(sc_tm[0:16, :, :], state_conv[l, :, :, :])], writes=scres)
            bk = psum()

            def fn():
                ins = None
                for r in range(2):
                    for j in range(8):
                        ins = tns.transpose(out=banks[bk][:, (r * 8 + j) * 16:(r * 8 + j + 1) * 16], in_=sc_tm[0:16, r, j * 128:(j + 1) * 128],
                                            identity=cst[0:16, O_ID:O_ID + 16])
                return ins
            S.op(PE, fn, reads=scres + [cst_res], writes=[bank_res[bk]])
            S.op(ACT, lambda: sca.copy(out=prevT[:, :, :].rearrange("p a s -> p (a s)"), in_=banks[bk][:, 0:256]), reads=[bank_res[bk]], writes=pres)
            dres = Res("sconv_copy%d" % l)
            S.dma(SP, dres, [(s_conv[l, :, 0, :], state_conv[l, :, 1, :])], is_output=True)
        yi = 0
        pend_t = []
        for j in range(8):
            w = wload([w_in[l, :, C_GC + j * 128:C_GC + (j + 1) * 128], w_in[l, :, C_H + j * 128:C_H + (j + 1) * 128],
                       w_in[l, :, C_GB + j * 128:C_GB + (j + 1) * 128]], KC)
            cw = lambda r: cwT[:, l * 24 + j * 3 + r:l * 24 + j * 3 + r + 1]
            if isA:
                S.op(DVE, lambda: vec.memset(U[:, 0:2], 0.0), writes=ures)
            else:
                S.op(ACT, lambda: sca.copy(out=U[:, 0:2], in_=carry[:, l, j, :]), reads=[carry_res[l][j]], writes=ures)
            for (c0, n) in H.tiles:
                bgc, bh, bgb = psum(), psum(), psum()
                xr = xn_reads(H, c0, n)
                rhs = [xnT[:, k, c0:c0 + n] for k in range(KC)]
                mm_group(bgc, 128, n, [w.ap[:, k, 0:128] for k in range(KC)], rhs, reads=[w.res] + xr)
                mm_group(bh, 128, n, [w.ap[:, k, 128:256] for k in range(KC)], rhs, reads=[w.res] + xr)
                mm_group(bgb, 128, n, [w.ap[:, k, 256:384] for k in range(KC)], rhs, reads=[w.res] + xr)
                while pend_t:
                    pend_t.pop(0)()
                gc = gcb[yi % 2]
                gcres = cells_of(gc)
                Y = Ybuf[yi % 2]
                yres = cells_of(Y)
                yi += 1
                S.op(ACT, lambda: sca.copy(out=gc[:, 0:n], in_=banks[bgc][:, 0:n]), reads=[bank_res[bgc]], writes=gcres)
                S.op(DVE, lambda: vec.tensor_tensor(out=U[:, 2 + c0:2 + c0 + n], in0=banks[bh][:, 0:n], in1=gc[:, 0:n], op=ALU.mult),
                     reads=[bank_res[bh]] + gcres, writes=ures)
                S.op(ACT, lambda: sca.activation(out=Y[:, 0:n], in_=U[:, c0:c0 + n], func=AF.Copy, scale=cw(0)), reads=ures + [cst_res], writes=yres)
                S.op(DVE, lambda: vec.scalar_tensor_tensor(out=Y[:, 0:n], in0=U[:, c0 + 1:c0 + 1 + n], scalar=cw(1), in1=Y[:, 0:n],
                                                           op0=ALU.mult, op1=ALU.add), reads=ures + yres + [cst_res], writes=yres)
                S.op(DVE, lambda: vec.scalar_tensor_tensor(out=Y[:, 0:n], in0=U[:, c0 + 2:c0 + 2 + n], scalar=cw(2), in1=Y[:, 0:n],
                                                           op0=ALU.mult, op1=ALU.add), reads=ures + yres + [cst_res], writes=yres)
                S.op(DVE, lambda: vec.tensor_tensor(out=Rt[:, j, c0:c0 + n], in0=banks[bgb][:, 0:n], in1=Y[:, 0:n], op=ALU.mult),
                     reads=[bank_res[bgb]] + yres, writes=[slab_res[j]])
                if isA and c0 == 0:
                    ysr = cells_of(Ys)
                    S.op(DVE, lambda: vec.tensor_scalar(out=Ys[:, :], in0=prevT[:, j, :], scalar1=cw(0), scalar2=None, op0=ALU.mult),
                         reads=pres + [cst_res], writes=ysr)
                    S.op(DVE, lambda: vec.scalar_tensor_tensor(out=Ys[:, :], in0=prevT[:, 8 + j, :], scalar=cw(1), in1=Ys[:, :],
                                                               op0=ALU.mult, op1=ALU.add), reads=pres + ysr + [cst_res], writes=ysr)
                    S.op(DVE, lambda: vec.scalar_tensor_tensor(out=Ys[:, :], in0=U[:, 2:18], scalar=cw(2), in1=Ys[:, :],
                                                               op0=ALU.mult, op1=ALU.add), reads=ures + ysr + [cst_res], writes=ysr)
                    S.op(DVE, lambda: vec.tensor_tensor(out=Rt[:, j, 0:16], in0=banks[bgb][:, 0:16], in1=Ys[:, :], op=ALU.mult),
                         reads=[bank_res[bgb]] + ysr, writes=[slab_res[j]])
            nc_ = H.ncols
            if isA:
                S.op(ACT, lambda: sca.copy(out=carry[:, l, j, :], in_=U[:, nc_:nc_ + 2]), reads=ures, writes=[carry_res[l][j]])
                def f_t(j=j):
                    bt = psum()
                    S.op(PE, lambda: tns.transpose(out=banks[bt][0:16, 0:128], in_=U[:, 2:18], identity=cst[:, O_ID:O_ID + 128]),
                         reads=ures + [cst_res], writes=[bank_res[bt]])
                    S.op(ACT, lambda: sca.copy(out=us_tm[0:16, 1, j * 128:(j + 1) * 128], in_=banks[bt][0:16, 0:128]),
                         reads=[bank_res[bt]], writes=scres)
                pend_t.append(f_t)
            else:
                def f_t(j=j, nc_=nc_):
                    bt = psum()
                    S.op(PE, lambda: tns.transpose(out=banks[bt][0:2, 0:128], in_=U[:, nc_:nc_ + 2], identity=cst[:, O_ID:O_ID + 128]),
                         reads=ures + [cst_res], writes=[bank_res[bt]])
                    S.op(ACT, lambda: sca.copy(out=pc_tm[0:2, j * 128:(j + 1) * 128], in_=banks[bt][0:2, 0:128]),
                         reads=[bank_res[bt]], writes=cells_of(pc_tm))
                pend_t.append(f_t)
        while pend_t:
            pend_t.pop(0)()
        if isA:
            S.dma(ACT, scres[0], [(s_conv[l, :, 1, :], us_tm[0:16, 1, :])], reads=scres, is_output=True)
        else:
            pcr = cells_of(pc_tm)
            S.dma(ACT, pcr[0], [(p_conv[l, :, :], pc_tm[0:2, :])], reads=pcr, is_output=True)

    def tm_out_phase(H, wmat_l, nk, src_slab0, npipe=None):
        npiece = (nk + 10) // 11 if nk > 8 else 1
        kper = nk // npiece

        def one(ws, q, bi, c0, rows):
            bk = psum()
            lhs = [Rt[:, src_slab0 + k, c0:c0 + rows] for k in range(nk)]
            rhs = [ws[k // kper].ap[:, k % kper, :] for k in range(nk)]
            mm_group(bk, rows, 256, lhs, rhs, reads=[w.res for w in ws] + [slab_res[src_slab0 + k] for k in range(nk)])
            S.op(DVE, lambda: vec.tensor_tensor(out=x_tm[0:rows, bi, q * 256:(q + 1) * 256], in0=banks[bk][0:rows, 0:256],
                                                in1=x_tm[0:rows, bi, q * 256:(q + 1) * 256], op=ALU.add),
                 reads=[bank_res[bk], x_res[bi]], writes=[x_res[bi]])
        if npiece == 1:
            wq = [[wload([wmat_l[0:nk * 128, q * 256:(q + 1) * 256]], nk)] for q in range(4)]
            for (bi, c0, rows) in H.blocks:
                for q in range(4):
                    one(wq[q], q, bi, c0, rows)
                if npipe is not None:
                    npipe.block(bi, c0, rows)
        else:
            for q in range(4):
                ws = [wload([wmat_l[(pp * kper) * 128:((pp + 1) * kper) * 128, q * 256:(q + 1) * 256]], kper) for pp in range(npiece)]
                for (bi, c0, rows) in H.blocks:
                    one(ws, q, bi, c0, rows)
                    if q == 3 and npipe is not None:
                        npipe.block(bi, c0, rows)
        if npipe is not None:
            npipe.flush()

    def ffn_up_phase(H, l):
        sres_all = [cells_of(t) for t in sgf]
        cnt = 0
        for j in range(NJ):
            w = wload([w_gu[l, :, j * 128:(j + 1) * 128], w_gu[l, :, DFF + j * 128:DFF + (j + 1) * 128]], KC)
            for (c0, n) in H.tiles:
                bg, bu = psum(), psum()
                xr = xn_reads(H, c0, n)
                rhs = [xnT[:, k, c0:c0 + n] for k in range(KC)]
                mm_group(bg, 128, n, [w.ap[:, k, 0:128] for k in range(KC)], rhs, reads=[w.res] + xr)
                mm_group(bu, 128, n, [w.ap[:, k, 128:256] for k in range(KC)], rhs, reads=[w.res] + xr)
                tb = sgf[cnt % 2]
                tres = sres_all[cnt % 2]
                cnt += 1
                S.op(ACT, lambda: sca.activation(out=tb[:, 0:n], in_=banks[bg][:, 0:n], func=AF.Silu), reads=[bank_res[bg]], writes=tres)
                S.op(DVE, lambda: vec.tensor_tensor(out=Rt[:, j, c0:c0 + n], in0=banks[bu][:, 0:n], in1=tb[:, 0:n], op=ALU.mult),
                     reads=[bank_res[bu]] + tres, writes=[slab_res[j]])

    for hn in halves:
        H = Half(hn)
        load_x(H)
        norm_phase(H, norm_mix[0:1, :], 0)
        for l in range(depth):
            gla_phase(H, l)
            yb_phase(H, l)
            conv_phase(H, l)
            gate_out_phase(H, l, w_a, C_GA, final=True)
            tm_out_phase(H, w_o[l], 8, 8, NormPipe(H, norm_ffn[l:l + 1, :], wj=2 + l))
            ffn_up_phase(H, l)
            if l + 1 < depth:
                nxt = NormPipe(H, norm_mix[l + 1:l + 2, :], wj=l + 1)
            else:
                nxt = NormPipe(H, final_norm[0:1, :], final=True)
            tm_out_phase(H, w_down[l], NJ, 0, nxt)
    S.finish()
    es.close()
    return nc


_NC_CACHE = {}


def kernel(x_prompt, x_sample, state_conv, state_gla, meta_tokens, w_in, conv_w, w_gk2, b_gk2,
           gla_gain, w_a_out, w_b_out, w_o, norm_mix, norm_ffn, w_gu, w_down, final_norm):
    f = lambda a: np.ascontiguousarray(np.asarray(a, dtype=np.float32))
    x_prompt, x_sample, state_conv, state_gla = f(x_prompt), f(x_sample), f(state_conv), f(state_gla)
    conv_wt = f(f(conv_w).reshape(2, 3, 8, 128).transpose(3, 0, 2, 1).reshape(128, 48))
    wgk_aug = f(np.concatenate([f(w_gk2).transpose(1, 0, 2).reshape(16, 1024), f(b_gk2).reshape(1, 1024)], axis=0))
    shared = {
        "meta_tokens": f(meta_tokens), "w_in": f(w_in), "conv_wt": conv_wt, "wgk_aug": wgk_aug, "gla_gain": f(gla_gain).reshape(1, 512),
        "w_a_out": f(w_a_out), "w_b_out": f(w_b_out), "w_o": f(w_o), "norm_mix": f(norm_mix), "norm_ffn": f(norm_ffn),
        "w_gu": f(w_gu), "w_down": f(w_down), "final_norm": f(final_norm).reshape(1, D), "consts": make_consts(),
        "normw_t": f(np.concatenate([f(norm_mix), f(norm_ffn)], axis=0).reshape(4, 8, 128).transpose(2, 0, 1).reshape(128, 32)),
    }
    in_maps = []
    for c in range(NCORE):
        m = dict(shared)
        m["x_prompt"] = f(x_prompt[c])
        m["x_sample"] = f(x_sample[16 * c:16 * (c + 1), 0, :])
        m["state_conv"] = f(state_conv[:, 16 * c:16 * (c + 1)])
        m["state_gla"] = f(state_gla[:, 16 * c:16 * (c + 1)])
        in_maps.append(m)
    if "nc" not in _NC_CACHE:
        _NC_CACHE["nc"] = build_nc()
    nc = _NC_CACHE["nc"]
    res = run_bass_kernel_spmd(nc, in_maps, core_ids=list(range(NCORE)))
    R = res.results
    y_prompt = np.stack([R[c]["y_prompt"] for c in range(NCORE)], axis=0)
    y_sample = np.concatenate([R[c]["y_sample"] for c in range(NCORE)], axis=0).reshape(128, 1, D)
    p_conv = np.stack([R[c]["p_conv"] for c in range(NCORE)], axis=1)
    p_gla = np.stack([R[c]["p_gla"] for c in range(NCORE)], axis=1)
    s_conv = np.concatenate([R[c]["s_conv"] for c in range(NCORE)], axis=1)
    s_gla = np.concatenate([R[c]["s_gla"] for c in range(NCORE)], axis=1)
    return (y_prompt.astype(np.float32), y_sample.astype(np.float32), p_conv.astype(np.float32),
            p_gla.astype(np.float32), s_conv.astype(np.float32), s_gla.astype(np.float32))
```
